# Optimizing a Trainium2 kernel written in Bass

```python
import jax, jax.numpy as jnp
from jax import lax
import numpy as np

D_MODEL = 4096
BATCH = 4
SEQ = 4096
DEPTH = 1

HEAD_DIM = 64
N_HEADS = D_MODEL // 128
N_KV_HEADS = N_HEADS // 8
GROUP = N_HEADS // N_KV_HEADS
WINDOW = 128
BLOCK = WINDOW
ROPE_THETA = 10000.0
CONV_DIM = D_MODEL // 2
CONV_WIDTH = 31
D_FF = 4 * D_MODEL
EPS = 1e-6

Q_DIM = N_HEADS * HEAD_DIM
KV_DIM = N_KV_HEADS * HEAD_DIM
OFF_Q = 0
OFF_K = OFF_Q + Q_DIM
OFF_V = OFF_K + KV_DIM
OFF_CONV = OFF_V + KV_DIM
OFF_GC = OFF_CONV + 2 * CONV_DIM
OFF_GA = OFF_GC + D_MODEL
IN_WIDTH = OFF_GA + D_MODEL

kernel_name = "hybrid_conformer_conv_swa_sink_gqa_block"


def rms_norm(x, g):
    xf = x.astype(jnp.float32)
    y = xf * lax.rsqrt(jnp.mean(xf * xf, axis=-1, keepdims=True) + EPS)
    return (y * g.astype(jnp.float32)).astype(x.dtype)


def layer_norm(x, g, b):
    xf = x.astype(jnp.float32)
    mu = jnp.mean(xf, axis=-1, keepdims=True)
    var = jnp.mean(jnp.square(xf - mu), axis=-1, keepdims=True)
    y = (xf - mu) * lax.rsqrt(var + EPS)
    return (y * g.astype(jnp.float32) + b.astype(jnp.float32)).astype(x.dtype)


def rope(t, pos):
    half = HEAD_DIM // 2
    inv_freq = ROPE_THETA ** (-jnp.arange(0, half, dtype=jnp.float32) / half)
    ang = pos.astype(jnp.float32)[:, None] * inv_freq[None, :]
    cos = jnp.cos(ang)[None, :, None, :]
    sin = jnp.sin(ang)[None, :, None, :]
    tf = t.astype(jnp.float32)
    t1, t2 = tf[..., :half], tf[..., half:]
    out = jnp.concatenate([t1 * cos - t2 * sin, t2 * cos + t1 * sin], axis=-1)
    return out.astype(t.dtype)


def conformer_conv(u, b_glu, w_dw, b_dw, ln_g, ln_b, w_conv_out):
    u = u + b_glu
    a, gate = u[..., :CONV_DIM], u[..., CONV_DIM:]
    glu = a * jax.nn.sigmoid(gate)
    dw = lax.conv_general_dilated(
        glu, w_dw[:, None, :].astype(glu.dtype), window_strides=(1,),
        padding=[(CONV_WIDTH - 1, 0)],
        dimension_numbers=("NWC", "WIO", "NWC"),
        feature_group_count=CONV_DIM) + b_dw
    z = jax.nn.silu(layer_norm(dw, ln_g, ln_b))
    return z @ w_conv_out


def swa_sink_attention(q, k, v, sinks, w_attn_out):
    b, s = q.shape[0], q.shape[1]
    nblk = s // BLOCK
    qb = q.reshape(b, nblk, BLOCK, N_KV_HEADS, GROUP, HEAD_DIM)
    kb = k.reshape(b, nblk, BLOCK, N_KV_HEADS, HEAD_DIM)
    vb = v.reshape(b, nblk, BLOCK, N_KV_HEADS, HEAD_DIM)
    pad = jnp.zeros_like(kb[:, :1])
    k2 = jnp.concatenate([jnp.concatenate([pad, kb[:, :-1]], axis=1), kb], axis=2)
    v2 = jnp.concatenate([jnp.concatenate([pad, vb[:, :-1]], axis=1), vb], axis=2)
    scale = HEAD_DIM ** -0.5
    scores = jnp.einsum("bnqhgd,bnkhd->bnhgqk", qb, k2).astype(jnp.float32) * scale
    blk = jnp.arange(nblk, dtype=jnp.int32)[:, None, None]
    qpos = blk * BLOCK + jnp.arange(BLOCK, dtype=jnp.int32)[None, :, None]
    kpos = (blk - 1) * BLOCK + jnp.arange(2 * BLOCK, dtype=jnp.int32)[None, None, :]
    diff = qpos - kpos
    valid = (diff >= 0) & (diff < WINDOW) & (kpos >= 0)
    scores = jnp.where(valid[None, :, None, None, :, :], scores, jnp.float32(-1e30))
    sink = jnp.broadcast_to(
        sinks.astype(jnp.float32).reshape(1, 1, N_KV_HEADS, GROUP, 1, 1),
        scores.shape[:-1] + (1,))
    probs = jax.nn.softmax(jnp.concatenate([scores, sink], axis=-1), axis=-1)[..., :-1]
    out = jnp.einsum("bnhgqk,bnkhd->bnqhgd", probs.astype(v.dtype), v2)
    return out.reshape(b, s, Q_DIM) @ w_attn_out


def setup_inputs(seed: int = 0) -> dict:
    key = jax.random.key(seed)
    ks = jax.random.split(key, 18)
    f32 = jnp.float32
    nrm = lambda k, shape, scale: jax.random.normal(k, shape, f32) * scale
    return {
        "x": nrm(ks[0], (BATCH, SEQ, D_MODEL), 1.0),
        "norm_mix_g": 1.0 + nrm(ks[1], (D_MODEL,), 0.02),
        "w_in": nrm(ks[2], (D_MODEL, IN_WIDTH), D_MODEL ** -0.5),
        "b_glu": nrm(ks[3], (2 * CONV_DIM,), 0.02),
        "w_dw": nrm(ks[4], (CONV_WIDTH, CONV_DIM), CONV_WIDTH ** -0.5),
        "b_dw": nrm(ks[5], (CONV_DIM,), 0.02),
        "conv_ln_g": 1.0 + nrm(ks[6], (CONV_DIM,), 0.02),
        "conv_ln_b": nrm(ks[7], (CONV_DIM,), 0.02),
        "w_conv_out": nrm(ks[8], (CONV_DIM, D_MODEL), CONV_DIM ** -0.5),
        "sinks": nrm(ks[9], (N_HEADS,), 0.5),
        "w_attn_out": nrm(ks[10], (Q_DIM, D_MODEL), Q_DIM ** -0.5),
        "w_out": nrm(ks[11], (D_MODEL, D_MODEL), D_MODEL ** -0.5),
        "norm_mlp_g": 1.0 + nrm(ks[12], (D_MODEL,), 0.02),
        "w_mlp_up": nrm(ks[13], (D_MODEL, D_FF), D_MODEL ** -0.5),
        "w_mlp_down": nrm(ks[14], (D_FF, D_MODEL), D_FF ** -0.5),
        "norm_final_g": 1.0 + nrm(ks[15], (D_MODEL,), 0.02),
    }


def reference(x, norm_mix_g, w_in, b_glu, w_dw, b_dw, conv_ln_g, conv_ln_b, w_conv_out,
              sinks, w_attn_out, w_out, norm_mlp_g, w_mlp_up, w_mlp_down, norm_final_g):
    b, s, _ = x.shape
    pos = jnp.arange(s, dtype=jnp.int32)
    for _layer in range(DEPTH):
        h = rms_norm(x, norm_mix_g)
        p = h @ w_in
        q = rope(p[..., OFF_Q:OFF_K].reshape(b, s, N_HEADS, HEAD_DIM), pos)
        k = rope(p[..., OFF_K:OFF_V].reshape(b, s, N_KV_HEADS, HEAD_DIM), pos)
        v = p[..., OFF_V:OFF_CONV].reshape(b, s, N_KV_HEADS, HEAD_DIM)
        conv_out = conformer_conv(p[..., OFF_CONV:OFF_GC], b_glu, w_dw, b_dw,
                                  conv_ln_g, conv_ln_b, w_conv_out)
        attn_out = swa_sink_attention(q, k, v, sinks, w_attn_out)
        merged = (jax.nn.sigmoid(p[..., OFF_GC:OFF_GA]) * conv_out
                  + jax.nn.sigmoid(p[..., OFF_GA:IN_WIDTH]) * attn_out)
        x = x + merged @ w_out
        h2 = rms_norm(x, norm_mlp_g)
        x = x + jnp.square(jax.nn.relu(h2 @ w_mlp_up)) @ w_mlp_down
    return rms_norm(x, norm_final_g)
```

```python
import numpy as np
import ml_dtypes
from contextlib import ExitStack
from collections import defaultdict

import concourse.bass as bass
import concourse.mybir as mybir
from concourse.bass_utils import run_bass_kernel_spmd

F32 = mybir.dt.float32
BF16 = mybir.dt.bfloat16
AF = mybir.ActivationFunctionType
ALU = mybir.AluOpType
AX = mybir.AxisListType

D = 4096
NQH = 32
OFF_Q, OFF_K, OFF_V, OFF_CONV, OFF_GC, OFF_GA = 0, 2048, 2304, 2560, 6656, 10752
IN_W = 14848
DFF = 16384
CW = 31
EPS = 1e-6
TT = 512
NSLOT = 4
MASKNEG = -30000.0

PC_G1, PC_G2, PC_BGLU, PC_BDW, PC_LNG, PC_LNB, PC_SINK, PC_WDW, PC_FLAG, PC_N = 0, 32, 64, 96, 112, 128, 144, 176, 672, 673


ALLW = ("w_in", "w_conv_out", "w_attn_out", "w_out", "w_mlp_up", "w_mlp_down")


def _needed_weights(dbg):
    if dbg in ("io", "rms"):
        return ()
    if dbg in ("kv", "conv", "ln", "qa"):
        return ("w_in",)
    if dbg == "merge":
        return ("w_in", "w_conv_out", "w_attn_out")
    if dbg == "wout":
        return ("w_in", "w_conv_out", "w_attn_out", "w_out")
    return ALLW


class Res:
    __slots__ = ("name", "w", "r", "parent", "children")

    def __init__(self, name, parent=None):
        self.name = name
        self.w = None
        self.r = []
        self.parent = parent
        self.children = []
        if parent is not None:
            parent.children.append(self)


ENGS = ("pe", "act", "dve", "pool", "sp")


class Sched:
    def __init__(self):
        self.lists = {e: [] for e in ENGS}
        self.cnt = defaultdict(int)
        self.seen = {e: defaultdict(int) for e in ENGS}
        self.sems = {}
        self.dma_keys = set()

    def _wait(self, eng, ev):
        key, val = ev
        if key in self.dma_keys:
            val = self.cnt[key]
        if self.seen[eng][key] >= val:
            return
        self.seen[eng][key] = val
        self.lists[eng].append(lambda E, key=key, val=val: E.wait_ge(self.sems[key], val))

    def op(self, eng, fn, reads=(), writes=(), mark=True, dma=None, nowait=False):
        excl = [r for r in reads if r.name.startswith("PB")]
        if excl:
            reads = [r for r in reads if not r.name.startswith("PB")]
            writes = list(writes) + excl
        deps = []
        for r in reads:
            for q in (r, r.parent):
                if q is not None and q.w is not None:
                    deps.append((q.w, "raw"))
            for c in r.children:
                if c.w is not None:
                    deps.append((c.w, "raw"))
        for w in writes:
            fam = [w] + ([w.parent] if w.parent is not None else []) + list(w.children)
            for q in fam:
                if q.w is not None:
                    deps.append((q.w, "waw"))
                for ev in q.r:
                    deps.append((ev, "war"))
        if nowait:
            deps = []
        for ev, kind in deps:
            if ev[0] == eng and eng == "pe":
                continue
            self._wait(eng, ev)
        if dma is not None:
            self.dma_keys.add(dma)
            self.cnt[dma] += 16
            ev = (dma, self.cnt[dma])
            self.lists[eng].append(lambda E, fn=fn, dma=dma: fn(E).then_inc(self.sems[dma], 16))
        elif mark:
            self.cnt[eng] += 1
            ev = (eng, self.cnt[eng])
            self.lists[eng].append(lambda E, fn=fn, eng=eng: fn(E).then_inc(self.sems[eng], 1))
        else:
            ev = (eng, self.cnt[eng] + 1)
            self.lists[eng].append(lambda E, fn=fn: fn(E))
        for r in reads:
            r.r.append(ev)
            if len(r.r) > 24:
                r.r = r.r[-24:] if False else self._compact(r.r)
        for w in writes:
            w.w = ev
            w.r = []

    @staticmethod
    def _compact(evs):
        best = {}
        for k, v in evs:
            if v > best.get(k, -1):
                best[k] = v
        return list(best.items())


def build_program(NT=4, dbg=None):
    NTOK = NT * TT
    nc = bass.Bass("TRN2", target_bir_lowering=False)

    def din(name, shape, dt=F32):
        return nc.dram_tensor(name, shape, dt, kind="ExternalInput").ap()

    x_d = din("x", [NTOK, D])
    xh_d = din("xh", [128, D])
    cos_d = din("cos_t", [128, 128 + NTOK])
    sin_d = din("sin_t", [128, 128 + NTOK])
    mask0_d = din("mask0", [128, 256], BF16)
    maskn_d = din("maskn", [128, 256], BF16)
    ident_d = din("ident", [128, 128], BF16)
    rperm_d = din("rperm", [128, 128], BF16)
    ones_d = din("ones", [128, 128], BF16)
    pcols_d = din("pcols", [128, PC_N])
    gfin_d = din("gfin", [128, D])
    fake = bool(dbg) and dbg.endswith("_fake")
    if fake:
        dbg = dbg[:-5]
    need = _needed_weights(dbg) if not fake else ()
    FSH = [4096, 512]
    KVL = 9
    import os
    KVFIX = os.environ.get('KVFIX') == '1'
    if dbg and dbg.startswith("kv") and len(dbg) == 3:
        KVL = int(dbg[2])
        dbg = "kv"
    w_in_d = din("w_in", [D, IN_W] if "w_in" in need else (FSH if fake else [128, 128]))
    w_co_d = din("w_conv_out", [2048, D] if "w_conv_out" in need else (FSH if fake else [128, 128]))
    w_ao_d = din("w_attn_out", [2048, D] if "w_attn_out" in need else (FSH if fake else [128, 128]))
    w_out_d = din("w_out", [D, D] if "w_out" in need else (FSH if fake else [128, 128]))
    w_up_d = din("w_mlp_up", [D, DFF] if "w_mlp_up" in need else (FSH if fake else [128, 128]))
    w_dn_d = din("w_mlp_down", [DFF, D] if "w_mlp_down" in need else (FSH if fake else [128, 128]))
    y_d = nc.dram_tensor("y", [NTOK, D], F32, kind="ExternalOutput").ap()
    NUMAX = 0
    UPS = 110
    wscr_parts = [nc.dram_tensor(f"wscr{k}", [UPS * 128, 4096], BF16).ap() for k in range(NUMAX // UPS)]

    def wscr_unit(u):
        return wscr_parts[u // UPS][(u % UPS) * 128:(u % UPS + 1) * 128, :]

    w_in_r = w_in_d.rearrange("(kc p) n -> p kc n", p=128)
    w_co_r = w_co_d.rearrange("(kc p) n -> p kc n", p=128)
    w_ao_r = w_ao_d.rearrange("(kc p) n -> p kc n", p=128)
    w_out_r = w_out_d.rearrange("(kc p) n -> p kc n", p=128)
    w_up_r = w_up_d.rearrange("(kc p) n -> p kc n", p=128)
    w_dn_r = w_dn_d.rearrange("(kc p) n -> p kc n", p=128)

    class _FakeView:
        def __init__(self, r):
            self.r = r

        def __getitem__(self, key):
            p, kc, cc = key
            nk = kc.stop - kc.start
            ncol = cc.stop - cc.start
            k0 = kc.start % (32 - nk + 1)
            c0 = cc.start % (512 - ncol + 1)
            return self.r[p, k0:k0 + nk, c0:c0 + ncol]

    if fake:
        w_in_r, w_co_r, w_ao_r, w_out_r, w_up_r, w_dn_r = [_FakeView(r) for r in (w_in_r, w_co_r, w_ao_r, w_out_r, w_up_r, w_dn_r)]

    S = Sched()
    es = ExitStack()

    def sb(name, shape, dt):
        return es.enter_context(nc.sbuf_tensor(name, shape, dt))

    R1 = sb("R1", [128, 4 * D], F32)
    R2 = sb("R2", [128, 32 * TT], BF16)
    R3 = sb("R3", [128, 32 * TT], BF16)
    WS = [sb(f"ws{i}", [128, 4096], BF16) for i in range(NSLOT)]
    KT = sb("KT", [128, 4 * 640], BF16)
    VD = sb("VD", [128, 5 * 512], BF16)
    QT = sb("QT", [128, 2 * 2 * TT], BF16)
    IDENT = sb("IDENT", [128, 128], BF16)
    RPERM = sb("RPERM", [128, 128], BF16)
    ONES = sb("ONES", [128, 128], BF16)
    MASKN = sb("MASKN", [128, 256], BF16)
    MASK0 = sb("MASK0", [128, 256], BF16)
    PCOLS = sb("PCOLS", [128, PC_N], F32)
    DCOLS = sb("DCOLS", [128, 80], F32)
    HB = sb("HB", [128, 16 * 30], F32)
    STAT = sb("STAT", [128, 64], F32)
    GLU = [sb(f"GLU{i}", [128, 30 + TT], F32) for i in range(2)]
    TA = [sb(f"TA{i}", [128, TT], F32) for i in range(2)]
    _tb = sb("TB0", [128, TT], F32)
    TB = [_tb, _tb]
    TC = [sb(f"TC{i}", [128, 2 * TT], BF16) for i in range(2)]
    E4 = [sb(f"E4{i}", [128, 1024], BF16) for i in range(2)]
    PT = [sb(f"PT{i}", [128, 1024], BF16) for i in range(2)]
    PS = es.enter_context(nc.psum_tensor("PS", [128, 8 * 512], F32))

    X1 = R1[:, :].rearrange("p (s d) -> p s d", s=4)
    DW = R1[:, 0:8192].rearrange("p (c n) -> p c n", c=16)
    ZT = R1[:, 8192:12288].bitcast(BF16).rearrange("p (c n) -> p c n", c=16)
    AT = R1[:, 12288:16384].bitcast(BF16).rearrange("p (c n) -> p c n", c=16)
    HT = R2[:, :].rearrange("p (c n) -> p c n", c=32)
    GFIN = R2[:, 0:8192].bitcast(F32)
    MG = R3[:, :].rearrange("p (c n) -> p c n", c=32)
    KTv = KT[:, :].rearrange("p (g n) -> p g n", g=4)
    VDv = VD[:, :].rearrange("p (b g o d) -> p b g o d", b=5, g=4, o=2)
    QTv = QT[:, :].rearrange("p (s c n) -> p s c n", s=2, c=2)
    HBv = HB[:, :].rearrange("p (c n) -> p c n", c=16)
    WDW = PCOLS[:, PC_WDW:PC_WDW + 496]
    XS = R3[:, 24 * TT:32 * TT]
    COS = R3[:, 16 * TT:18 * TT].bitcast(F32)
    SIN = R3[:, 18 * TT:20 * TT].bitcast(F32)
    MEAN = R3[:, 20 * TT:22 * TT].bitcast(F32)
    RSTD = R3[:, 22 * TT:24 * TT].bitcast(F32)

    def bank(b, n=1):
        return PS[:, b * 512:(b + n) * 512]

    def bank_bf(b):
        return PS[:, b * 512:(b + 1) * 512].bitcast(BF16)

    rRX = [Res(f"RX{s}") for s in range(4)]
    rDWh = [[Res(f"DW{c}_{h}", rRX[c // 8]) for h in range(2)] for c in range(16)]
    rZT = [Res(f"ZT{c}", rRX[2]) for c in range(16)]
    rAT = Res("AT", rRX[3])
    rXP = [[Res(f"XP{s}_{d}", rRX[s]) for d in range(8)] for s in range(4)]
    rR2 = Res("R2")
    rMG = Res("MG")
    rMGc = [Res(f"MG{m}", rMG) for m in range(32)]
    rWS = [Res(f"WS{i}") for i in range(NSLOT)]
    rKT = Res("KT")
    rVD = Res("VD")
    rQT = [Res("QT0"), Res("QT1")]
    rCONST = Res("CONST")
    rDC = Res("DCOLS")
    rHB = [Res(f"HB{c}") for c in range(16)]
    rGLU = [Res("GLU0"), Res("GLU1")]
    rTA = [Res("TA0"), Res("TA1")]
    _rtb = Res("TB0")
    rTB = [_rtb, _rtb]
    rTC = [Res("TC0"), Res("TC1")]
    rE4 = [Res("E40"), Res("E41")]
    rPT = [Res("PT0"), Res("PT1")]
    rPB = [Res(f"PB{b}") for b in range(8)]
    rMEAN = rMGc[20:22]
    rRSTD = rMGc[22:24]
    XSR = rMGc[24:32]
    XSB = [XS, R3[:, 8 * TT:16 * TT]]
    XSBR = [rMGc[24:32], rMGc[8:16]]
    JUNK = R3[:, 0:8 * TT]
    JUNKR = rMGc[0:8]
    rSS = [Res(f"SS{i}") for i in range(4)]
    rSD = [Res(f"SD{i}") for i in range(4)]
    rRS = [Res(f"RS{i}") for i in range(4)]
    CSR = rMGc[16:20]
    rST = [Res(f"ST{i}") for i in range(8)]
    rMX = [Res("MX0"), Res("MX1")]
    rNB = [Res("NB0"), Res("NB1")]
    rRSUM = [Res("RS0"), Res("RS1")]
    rSK = [Res("SK0"), Res("SK1")]

    def ld(dst_ap, src_ap, res, key="ld", eng="sp"):
        S.op(eng, lambda E: E.dma_start(out=dst_ap, in_=src_ap), writes=[res], dma=key)

    def ldm(dst_ap, src_ap, ress, key="ld", eng="sp"):
        S.op(eng, lambda E: E.dma_start(out=dst_ap, in_=src_ap), writes=list(ress), dma=key)

    for dst, src in ((IDENT, ident_d), (RPERM, rperm_d), (ONES, ones_d), (MASKN, maskn_d),
                     (MASK0, mask0_d), (PCOLS, pcols_d)):
        ld(dst[:, :], src[:, :], rCONST)
    S.op("dve", lambda E: E.tensor_scalar(DCOLS[:, 0:32], PCOLS[:, PC_BGLU:PC_BGLU + 32], 0.5, None, ALU.mult),
         reads=[rCONST], writes=[rDC])
    S.op("dve", lambda E: E.tensor_scalar(DCOLS[:, 32:64], PCOLS[:, PC_LNG:PC_LNG + 32], 0.5, None, ALU.mult),
         reads=[rCONST], writes=[rDC])
    S.op("dve", lambda E: E.reduce_max(DCOLS[:, 65:66], PCOLS[:, PC_SINK:PC_SINK + 32], AX.X),
         reads=[rCONST], writes=[rDC])
    S.op("dve", lambda E: E.tensor_scalar(DCOLS[:, 64:65], DCOLS[:, 65:66], -1.0, None, ALU.mult),
         reads=[rDC], writes=[rDC])
    HBGLU = DCOLS[:, 0:32]
    HLNG = DCOLS[:, 32:48]
    HLNB = DCOLS[:, 48:64]
    NSM = DCOLS[:, 64:65]
    G1 = PCOLS[:, PC_G1:PC_G1 + 32]
    G2 = PCOLS[:, PC_G2:PC_G2 + 32]
    BDW = PCOLS[:, PC_BDW:PC_BDW + 16]
    SINK = PCOLS[:, PC_SINK:PC_SINK + 32]
    FLAG = PCOLS[:, PC_FLAG:PC_FLAG + 1]

    ws_state = {"i": 0, "mode": "cast", "u": 0}
    rSCR = [Res(f"SCR{u}") for u in range(NUMAX)]

    def load_unit(pieces):
        i = ws_state["i"] % NSLOT
        ws_state["i"] += 1
        slot, res = WS[i], rWS[i]
        mode = ws_state["mode"]
        u = ws_state["u"]
        if mode != "cast":
            ws_state["u"] += 1
            assert u < NUMAX
        if mode == "consume":
            S.op("pool", lambda E, u=u: E.dma_start(out=slot[:, :], in_=wscr_unit(u)),
                 reads=[rSCR[u]], writes=[res], dma=f"w{i}")
            return slot, res
        for pi, (dst_fn, src) in enumerate(pieces):
            dst = dst_fn(slot)
            S.op("pool", lambda E, dst=dst, src=src: E.dma_start(out=dst, in_=src), writes=[res], dma=f"w{i}",
                 nowait=(pi > 0))
        if mode == "produce":
            S.op("sp", lambda E, u=u: E.dma_start(out=wscr_unit(u), in_=slot[:, :]),
                 reads=[res], writes=[rSCR[u]], dma="ws")
        return slot, res

    def unit_fm(src_r, kc0, c0):
        return load_unit([(lambda s: s[:, :].rearrange("p (kc n) -> p kc n", kc=16),
                           src_r[:, kc0:kc0 + 16, c0:c0 + 256])])

    def unit_tm(src_r, kc0, c0):
        return load_unit([(lambda s: s[:, :].rearrange("p (kc n) -> p kc n", kc=8),
                           src_r[:, kc0:kc0 + 8, c0:c0 + 512])])

    pa_state = {"i": 0}

    def next_pa():
        b = pa_state["i"] % 4
        pa_state["i"] += 1
        return b

    def proj_fm_gen(unit_fns, nk, rhs_fn, rhs_res, ntok, banks):
        nu = len(unit_fns)
        for u, ufn in enumerate(unit_fns):
            slot, res = ufn()
            uv = slot[:, :].rearrange("p (kc n) -> p kc n", kc=16)
            for j in range(2):
                for kc in range(16):
                    k = u * 16 + kc
                    last = (u == nu - 1 and kc == 15)
                    S.op("pe", lambda E, j=j, kc=kc, k=k, uv=uv, last=last:
                         E.matmul(bank(banks[j])[:, 0:ntok], uv[:, kc, j * 128:(j + 1) * 128], rhs_fn(k),
                                  start=(k == 0), stop=last),
                         reads=[res, rhs_res], writes=[rPB[banks[j]]],
                         mark=(last or (j == 1 and kc == 15)))
                yield

    def proj_fm(units, nk, rhs_fn, rhs_res, ntok, banks):
        for _ in proj_fm_gen([(lambda x=x: x) for x in units], nk, rhs_fn, rhs_res, ntok, banks):
            pass

    def rms_to_fm(nsub, gcols, tokoff_fn=None):
        def st1(s):
            ss = STAT[:, s:s + 1]
            sd = STAT[:, 8 + s:9 + s]
            rs = STAT[:, 16 + s:17 + s]
            xs, xr = XSB[s % 2], XSBR[s % 2]
            S.op("act", lambda E, s=s, ss=ss: E.activation(JUNK[:, :], X1[:, s, :], AF.Square, accum_out=ss),
                 reads=[rRX[s]], writes=JUNKR + [rSS[s]])
            S.op("act", lambda E, ss=ss, sd=sd: E.activation(sd, ss, AF.Sqrt, bias=EPS_COL[:, 0:1], scale=1.0 / D),
                 reads=[rSS[s], rCONST], writes=[rSD[s]])
            S.op("dve", lambda E, sd=sd, rs=rs: E.reciprocal(rs, sd), reads=[rSD[s]], writes=[rRS[s]])
            S.op("dve", lambda E, s=s, rs=rs, xs=xs: E.tensor_scalar(xs[:, :], X1[:, s, :], rs, None, ALU.mult),
                 reads=[rRX[s], rRS[s]], writes=xr)

        def st2(s):
            xs, xr = XSB[s % 2], XSBR[s % 2]
            for cg in range(4):
                b = next_pa()
                pv = bank_bf(b).rearrange("p (c n) -> p c n", c=8)
                for ci in range(8):
                    c = cg * 8 + ci
                    S.op("pe", lambda E, pv=pv, ci=ci, c=c, xs=xs: E.transpose(pv[:, ci, :], xs[:, c * 128:(c + 1) * 128], IDENT[:, :]),
                         reads=xr + [rCONST], writes=[rPB[b]], mark=(ci == 7))
                for ci in range(8):
                    c = cg * 8 + ci
                    S.op("act", lambda E, pv=pv, ci=ci, c=c, s=s:
                         E.activation(HT[:, c, s * 128:(s + 1) * 128], pv[:, ci, :], AF.Identity, scale=gcols[:, c:c + 1]),
                         reads=[rPB[b], rCONST], writes=[rR2])

        st1(0)
        for s in range(nsub):
            if s + 1 < nsub:
                st1(s + 1)
            st2(s)

    def rope(b_raw, dst_ap, ntok, dst_res, csoff, b_rot=2):
        ti = rope_state["i"] % 2
        rope_state["i"] += 1
        qb = TC[ti][:, 0:ntok]
        t1 = TA[ti][:, 0:ntok]
        t2 = TB[ti][:, 0:ntok]
        raw = bank(b_raw)[:, 0:ntok]
        rot = bank(b_rot)[:, 0:ntok]
        S.op("act", lambda E: E.activation(qb, raw, AF.Identity), reads=[rPB[b_raw]], writes=[rTC[ti]])
        S.op("pe", lambda E: E.matmul(rot, RPERM[:, :], qb, start=True, stop=True),
             reads=[rTC[ti], rCONST], writes=[rPB[b_rot]])
        S.op("dve", lambda E: E.tensor_tensor(t1, raw, COS[:, csoff:csoff + ntok], ALU.mult),
             reads=[rPB[b_raw]] + CSR, writes=[rTA[ti]])
        S.op("dve", lambda E: E.tensor_tensor(t2, rot, SIN[:, csoff:csoff + ntok], ALU.mult),
             reads=[rPB[b_rot]] + CSR, writes=[rTB[ti]])
        S.op("dve", lambda E: E.tensor_tensor(dst_ap, t1, t2, ALU.add),
             reads=[rTA[ti], rTB[ti]], writes=[dst_res])

    rope_state = {"i": 0}

    def phase_kv(ntok, nsub, koff, vb0):
        rhs_fn = lambda k: HT[:, k, 0:ntok]
        for gp in range(2):
            units = []
            for u in range(2):
                pieces = []
                for o in range(2):
                    for gg in range(2):
                        c0 = OFF_K + gp * 128 + gg * 64
                        pieces.append((lambda s, o=o, gg=gg: s[:, :].rearrange("p (kc g o d) -> p kc g o d", kc=16, g=2, o=2)[:, :, gg, o, :],
                                       w_in_r[:, u * 16:(u + 1) * 16, c0:c0 + 64]))
                units.append(load_unit(pieces))
            if KVL == 1:
                return
            banks = [next_pa(), next_pa()] if not KVFIX else [0, 1]
            proj_fm(units, 32, rhs_fn, rR2, ntok, banks)
            if KVL == 2:
                return
            for j in range(2):
                g = gp * 2 + j
                if KVL == 7 or (KVL == 8 and gp == 0):
                    continue
                rope(banks[j], KTv[:, g, koff:koff + ntok], ntok, rKT, 0, b_rot=5)
            if KVL == 3:
                return
        if KVL in (4, 7, 8):
            return
        units = [unit_fm(w_in_r, u * 16, OFF_V) for u in range(2)]
        vbanks = [next_pa() for _ in range(nsub)]
        for u, (slot, res) in enumerate(units):
            uv = slot[:, :].rearrange("p (kc n) -> p kc n", kc=16)
            for blk in range(nsub):
                for kc in range(16):
                    k = u * 16 + kc
                    S.op("pe", lambda E, blk=blk, kc=kc, k=k, uv=uv:
                         E.matmul(bank(vbanks[blk])[:, 0:256], HT[:, k, blk * 128:(blk + 1) * 128], uv[:, kc, :],
                                  start=(k == 0), stop=(k == 31)),
                         reads=[res, rR2], writes=[rPB[vbanks[blk]]], mark=(kc == 15))
        if KVL == 5:
            return
        for blk in range(nsub):
            src = bank(vbanks[blk])[:, 0:256].rearrange("p (g d) -> p g d", g=4)
            for o in range(2):
                S.op("act", lambda E, blk=blk, o=o, src=src: E.activation(VDv[:, vb0 + blk, :, o, :], src, AF.Identity),
                     reads=[rPB[vbanks[blk]]], writes=[rVD])

    def conv_pair_proj_gen(cp, ntok, halo):
        rhs_fn = lambda k: HT[:, k, 0:ntok]
        for gi in range(2):
            c = cp * 2 + gi
            unit_fns = []
            for u in range(2):
                unit_fns.append(lambda u=u, c=c: load_unit([(
                    lambda s, two=two: s[:, :].rearrange("p (kc two n) -> p kc two n", kc=16, two=2)[:, :, two, :],
                    w_in_r[:, u * 16:(u + 1) * 16, OFF_CONV + two * 2048 + c * 128:OFF_CONV + two * 2048 + (c + 1) * 128])
                    for two in range(2)]))
            banks = [6, 7] if gi == 0 else [0, 1]
            yield from proj_fm_gen(unit_fns, 32, rhs_fn, rR2, ntok, banks)
            glu = GLU[gi]
            th = TA[gi][:, 0:ntok]
            S.op("act", lambda E, th=th, c=c, b=banks[1]: E.activation(th, bank(b)[:, 0:ntok], AF.Tanh,
                                                                      bias=HBGLU[:, 16 + c:17 + c], scale=0.5),
                 reads=[rPB[banks[1]], rDC], writes=[rTA[gi]])
            S.op("act", lambda E, glu=glu, c=c, b=banks[0]: E.activation(glu[:, 30:30 + ntok], bank(b)[:, 0:ntok], AF.Identity,
                                                                        bias=HBGLU[:, c:c + 1], scale=0.5),
                 reads=[rPB[banks[0]], rDC], writes=[rGLU[gi]])
            if not halo:
                S.op("act", lambda E, glu=glu, c=c: E.activation(glu[:, 0:30], HBv[:, c, :], AF.Identity),
                     reads=[rHB[c]], writes=[rGLU[gi]])
            S.op("dve", lambda E, glu=glu, th=th: E.scalar_tensor_tensor(glu[:, 30:30 + ntok], th, 1.0, glu[:, 30:30 + ntok], ALU.add, ALU.mult),
                 reads=[rTA[gi], rGLU[gi]], writes=[rGLU[gi]])
            if halo:
                S.op("dve", lambda E, glu=glu, c=c: E.tensor_scalar(HBv[:, c, :], glu[:, ntok:ntok + 30], FLAG, None, ALU.mult),
                     reads=[rGLU[gi], rCONST], writes=[rHB[c]])
            else:
                S.op("act", lambda E, glu=glu, c=c: E.activation(HBv[:, c, :], glu[:, ntok:ntok + 30], AF.Identity),
                     reads=[rGLU[gi]], writes=[rHB[c]])

    def conv_pair_proj(cp, ntok, halo):
        for _ in conv_pair_proj_gen(cp, ntok, halo):
            pass

    def conv_pair_taps_gen(cp):
        for j in range(CW):
            if j > 0:
                yield
            for gi in range(2):
                c = cp * 2 + gi
                glu = GLU[gi]
                o = DW[:, c, :]
                g_in = glu[:, j:j + TT]
                wcol = WDW[:, c * 31 + j:c * 31 + j + 1]
                if j == 0:
                    S.op("dve", lambda E, o=o, g_in=g_in, wcol=wcol, c=c:
                         E.tensor_scalar(o, g_in, wcol, BDW[:, c:c + 1], ALU.mult, ALU.add),
                         reads=[rGLU[gi], rCONST], writes=rDWh[c])
                else:
                    S.op("dve", lambda E, o=o, g_in=g_in, wcol=wcol:
                         E.scalar_tensor_tensor(o, g_in, wcol, o, ALU.mult, ALU.add),
                         reads=[rGLU[gi], rCONST] + rDWh[c], writes=rDWh[c])

    def conv_pair_stats(cp):
        for gi in range(2):
            c = cp * 2 + gi
            dwb = TC[gi][:, 0:TT]
            sq = TC[gi][:, TT:2 * TT]
            S.op("act", lambda E, dwb=dwb, c=c: E.activation(dwb, DW[:, c, :], AF.Identity), reads=rDWh[c], writes=[rTC[gi]])
            S.op("act", lambda E, sq=sq, c=c: E.activation(sq, DW[:, c, :], AF.Square), reads=rDWh[c], writes=[rTC[gi]])
        for which in range(2):
            for gi in range(2):
                src = TC[gi][:, which * TT:(which + 1) * TT]
                S.op("pe", lambda E, src=src, which=which, gi=gi: E.matmul(bank(which), ONES[:, :], src, start=(gi == 0), stop=(gi == 1)),
                     reads=[rTC[gi], rCONST], writes=[rPB[which]], mark=(gi == 1))
        for which, (acc, racc) in enumerate(((MEAN, rMEAN), (RSTD, rRSTD))):
            if cp == 0:
                S.op("dve", lambda E, acc=acc, which=which: E.tensor_copy(acc[:, :], bank(which)), reads=[rPB[which]], writes=racc)
            else:
                S.op("dve", lambda E, acc=acc, which=which: E.tensor_tensor(acc[:, :], acc[:, :], bank(which), ALU.add),
                     reads=[rPB[which]] + racc, writes=racc)

    def phase_ln():
        msq = TA[0][:, :]
        S.op("dve", lambda E: E.tensor_scalar(MEAN[:, :], MEAN[:, :], 1.0 / 2048, None, ALU.mult), reads=rMEAN, writes=rMEAN)
        S.op("dve", lambda E: E.tensor_tensor(msq, MEAN[:, :], MEAN[:, :], ALU.mult), reads=rMEAN, writes=[rTA[0]])
        S.op("dve", lambda E: E.scalar_tensor_tensor(RSTD[:, :], RSTD[:, :], 1.0 / 2048, msq, ALU.mult, ALU.subtract),
             reads=rRSTD + [rTA[0]], writes=rRSTD)
        S.op("act", lambda E: E.activation(RSTD[:, :], RSTD[:, :], AF.Sqrt, bias=EPS_COL[:, 0:1], scale=1.0),
             reads=rRSTD + [rCONST], writes=rRSTD)
        S.op("dve", lambda E: E.reciprocal(RSTD[:, :], RSTD[:, :]), reads=rRSTD, writes=rRSTD)
        THB = [PT[0][:, :].bitcast(F32), PT[1][:, :].bitcast(F32)]
        YHB = [GLU[0][:, 0:TT], GLU[1][:, 0:TT]]

        def ln_a(c):
            i = c % 2
            t, th, yh = TA[i][:, :], THB[i], YHB[i]
            S.op("dve", lambda E, t=t, c=c: E.tensor_tensor(t, DW[:, c, :], MEAN[:, :], ALU.subtract),
                 reads=rDWh[c] + rMEAN, writes=[rTA[i]])
            S.op("dve", lambda E, t=t: E.tensor_tensor(t, t, RSTD[:, :], ALU.mult), reads=[rTA[i]] + rRSTD, writes=[rTA[i]])
            S.op("act", lambda E, t=t, th=th, c=c: E.activation(th, t, AF.Tanh, bias=HLNB[:, c:c + 1], scale=HLNG[:, c:c + 1]),
                 reads=[rTA[i], rDC], writes=[rPT[i]])
            S.op("dve", lambda E, t=t, yh=yh, c=c: E.tensor_scalar(yh, t, HLNG[:, c:c + 1], HLNB[:, c:c + 1], ALU.mult, ALU.add),
                 reads=[rTA[i], rDC], writes=[rGLU[i]])

        def ln_b(c):
            i = c % 2
            th, yh = THB[i], YHB[i]
            S.op("dve", lambda E, th=th, yh=yh, c=c: E.scalar_tensor_tensor(ZT[:, c, :], th, 1.0, yh, ALU.add, ALU.mult),
                 reads=[rGLU[i], rPT[i]], writes=[rZT[c]])

        ln_a(0)
        for c in range(16):
            if c + 1 < 16:
                ln_a(c + 1)
            ln_b(c)

    qa_k = {"k": 0}

    def qproj_gen(hq):
        slot = hq % 2
        unit_fns = [(lambda u=u: unit_fm(w_in_r, u * 16, OFF_Q + hq * 256)) for u in range(2)]
        banks = [0, 1]
        yield from proj_fm_gen(unit_fns, 32, lambda k: HT[:, k, :], rR2, TT, banks)
        for j in range(2):
            rope(banks[j], QTv[:, slot, j, :], TT, rQT[slot], 0, b_rot=6 + j)

    def chain_gens(*gens):
        for g in gens:
            if g is not None:
                yield from g

    def pull_gen(g, n):
        if g is None:
            return
        for _ in range(n):
            try:
                next(g)
            except StopIteration:
                return

    def quad_attn(hq, first_tile, filler=None, dfiller=None):
        rhs_fn = lambda k: HT[:, k, :]
        slot_idx = [0]

        def slot_fill():
            k = slot_idx[0]
            slot_idx[0] += 1
            if k == 4:
                pull_gen(dfiller, 1000)
            pull(2)

        def dfill(n=2):
            pull_gen(dfiller, n)

        def pull(n=1):
            if filler is None:
                return
            for _ in range(n):
                try:
                    next(filler)
                except StopIteration:
                    return

        def qproj(hq):
            slot = hq % 2
            units = [unit_fm(w_in_r, u * 16, OFF_Q + hq * 256) for u in range(2)]
            banks = [0, 1]
            proj_fm(units, 32, rhs_fn, rR2, TT, banks)
            for j in range(2):
                rope(banks[j], QTv[:, slot, j, :], TT, rQT[slot], 0)

        def stage_a(hq, n, k):
            slot = hq % 2
            g = hq // 2
            sb_ = 3
            e4 = E4[k % 2]
            mask = MASK0 if (first_tile and n == 0) else MASKN
            s4 = bank(sb_, 2)
            s4v = s4.rearrange("p (i n) -> p i n", i=4)
            for i in range(4):
                hf, ci = i // 2, i % 2
                p0 = hf * 64
                S.op("pe", lambda E, i=i, ci=ci, p0=p0, s4v=s4v:
                     E.matmul(s4v[:, i, :], QTv[p0:p0 + 64, slot, ci, n * 128:(n + 1) * 128],
                              KTv[p0:p0 + 64, g, n * 128:n * 128 + 256], start=True, stop=False),
                     reads=[rQT[slot], rKT], writes=[rPB[sb_], rPB[sb_ + 1]], mark=False)
                S.op("pe", lambda E, i=i, s4v=s4v, mask=mask:
                     E.matmul(s4v[:, i, :], IDENT[:, :], mask[:, :], start=False, stop=True),
                     reads=[rCONST], writes=[rPB[sb_], rPB[sb_ + 1]], mark=(i == 3))
            mx = STAT[:, 24 + (k % 2):25 + (k % 2)]
            nb = STAT[:, 26 + (k % 2):27 + (k % 2)]
            rsum = STAT[:, 32 + 4 * (k % 2):36 + 4 * (k % 2)]
            sk = STAT[:, 40 + 4 * (k % 2):44 + 4 * (k % 2)]
            rmx = rMX[k % 2]
            rnb = rNB[k % 2]
            rrs = rRSUM[k % 2]
            rsk = rSK[k % 2]
            S.op("dve", lambda E: E.reduce_max(mx, s4, AX.X), reads=[rPB[sb_], rPB[sb_ + 1]], writes=[rmx])
            S.op("dve", lambda E: E.tensor_scalar(nb, mx, -0.125, NSM, ALU.mult, ALU.min), reads=[rmx, rDC], writes=[rnb])
            dfill(4)
            e4v = e4[:, :].rearrange("p (i n) -> p i n", i=4)
            for i in range(4):
                S.op("act", lambda E, i=i: E.activation(e4v[:, i, :], s4v[:, i, :], AF.Exp, bias=nb, scale=0.125,
                                                        accum_out=rsum[:, i:i + 1]),
                     reads=[rPB[sb_], rPB[sb_ + 1], rnb], writes=[rE4[k % 2], rrs])
            S.op("act", lambda E: E.activation(sk.rearrange("p (h c) -> p h c", h=2),
                                               SINK[:, hq * 4:hq * 4 + 4].rearrange("p (c h) -> p h c", h=2),
                                               AF.Exp, bias=nb, scale=1.0),
                 reads=[rnb, rCONST], writes=[rsk])
            S.op("dve", lambda E: E.tensor_tensor(sk, sk, rsum, ALU.add), reads=[rsk, rrs], writes=[rsk])
            S.op("dve", lambda E: E.reciprocal(sk, sk), reads=[rsk], writes=[rsk])
            S.op("dve", lambda E: E.tensor_tensor(e4v, e4v, sk.unsqueeze(2).broadcast_to([128, 4, 256]), ALU.mult),
                 reads=[rE4[k % 2], rsk], writes=[rE4[k % 2]])
            dfill(4)

        def stage_b(hq, n, k):
            g = hq // 2
            e4v = E4[k % 2][:, :].rearrange("p (i n) -> p i n", i=4)
            ptp = bank_bf(5).rearrange("p (kb i n) -> p kb i n", kb=2, i=4)
            for i in range(4):
                for kb in range(2):
                    S.op("pe", lambda E, i=i, kb=kb: E.transpose(ptp[:, kb, i, :], e4v[:, i, kb * 128:(kb + 1) * 128], IDENT[:, :]),
                         reads=[rE4[k % 2], rCONST], writes=[rPB[5]], mark=(i == 3 and kb == 1))
            pt = PT[k % 2]
            S.op("act", lambda E: E.activation(pt[:, :], bank_bf(5), AF.Identity), reads=[rPB[5]], writes=[rPT[k % 2]])
            ptv = pt[:, :].rearrange("p (kb n) -> p kb n", kb=2)
            for kb in range(2):
                S.op("pe", lambda E, kb=kb: E.matmul(bank(2), VDv[:, n + kb, g, :, :].rearrange("p o d -> p (o d)"), ptv[:, kb, :],
                                                     start=(kb == 0), stop=(kb == 1)),
                     reads=[rVD, rPT[k % 2]], writes=[rPB[2]], mark=(kb == 1))
            ov = bank(2).rearrange("p (i n) -> p i n", i=4)
            for hf in range(2):
                p0 = hf * 64
                S.op("dve", lambda E, hf=hf, p0=p0: E.tensor_copy(
                    AT[p0:p0 + 64, 2 * hq:2 * hq + 2, n * 128:(n + 1) * 128],
                    ov[p0:p0 + 64, 2 * hf:2 * hf + 2, :]),
                     reads=[rPB[2]], writes=[rAT])
            dfill(3)

        prev = None
        for n in range(4):
            idx = qa_k["k"]
            qa_k["k"] += 1
            stage_a(hq, n, idx)
            slot_fill()
            if prev is not None:
                stage_b(*prev)
                slot_fill()
            prev = (hq, n, idx)
        stage_b(*prev)
        pull_gen(dfiller, 1000)
        pull(64)

    def merge_gates(mp):
        hrhs = lambda k: HT[:, k, :]
        ti = mp % 2
        tcv = TC[ti][:, :].rearrange("p (j n) -> p j n", j=2)
        tdv = E4[ti][:, :].rearrange("p (j n) -> p j n", j=2)
        units = [unit_fm(w_in_r, u * 16, OFF_GC + mp * 256) for u in range(2)]
        banks = [next_pa(), next_pa()]
        proj_fm(units, 32, hrhs, rR2, TT, banks)
        for j in range(2):
            S.op("act", lambda E, j=j, b=banks[j], tcv=tcv: E.activation(tcv[:, j, :], bank(b), AF.Tanh, scale=0.5),
                 reads=[rPB[banks[j]]], writes=[rTC[ti]])
        units = [unit_fm(w_in_r, u * 16, OFF_GA + mp * 256) for u in range(2)]
        banks = [next_pa(), next_pa()]
        proj_fm(units, 32, hrhs, rR2, TT, banks)
        for j in range(2):
            S.op("act", lambda E, j=j, b=banks[j], tdv=tdv: E.activation(tdv[:, j, :], bank(b), AF.Tanh, scale=0.5),
                 reads=[rPB[banks[j]]], writes=[rE4[ti]])

    def merge_outs(mp):
        zrhs = lambda k: ZT[:, k, :]
        arhs = lambda k: AT[:, k, :]
        ti = mp % 2
        tcv = TC[ti][:, :].rearrange("p (j n) -> p j n", j=2)
        tdv = E4[ti][:, :].rearrange("p (j n) -> p j n", j=2)
        units = [unit_fm(w_co_r, 0, mp * 256)]
        banks = [next_pa(), next_pa()]
        proj_fm(units, 16, zrhs, rRX[2], TT, banks)
        uu = [TA[ti][:, :], PT[ti][:, :].bitcast(F32)]
        ru = [rTA[ti], rPT[ti]]
        for j in range(2):
            S.op("dve", lambda E, j=j, b=banks[j], uu=uu, tcv=tcv: E.scalar_tensor_tensor(uu[j], tcv[:, j, :], 1.0, bank(b), ALU.add, ALU.mult),
                 reads=[rTC[ti], rPB[banks[j]]], writes=[ru[j]])
        units = [unit_fm(w_ao_r, 0, mp * 256)]
        banks = [next_pa(), next_pa()]
        proj_fm(units, 16, arhs, rRX[3], TT, banks)
        ww = [GLU[0][:, 0:TT], GLU[1][:, 0:TT]]
        rw = [rGLU[0], rGLU[1]]
        for j in range(2):
            m = mp * 2 + j
            S.op("dve", lambda E, j=j, b=banks[j], ww=ww, tdv=tdv: E.scalar_tensor_tensor(ww[j], tdv[:, j, :], 1.0, bank(b), ALU.add, ALU.mult),
                 reads=[rE4[ti], rPB[banks[j]]], writes=[rw[j]])
            S.op("dve", lambda E, j=j, m=m, uu=uu, ww=ww: E.tensor_tensor(MG[:, m, :], uu[j], ww[j], ALU.add),
                 reads=[ru[j], rw[j]], writes=[rMGc[m]])

    def phase_merge():
        for mp in range(16):
            if mp + 1 < 16:
                merge_gates(mp + 1)
            merge_outs(mp)

    def phase_wout(t):
        for s in range(2, 4):
            r0 = t * TT + s * 128
            S.op("sp", lambda E, s=s, r0=r0: E.dma_start(out=X1[:, s, :], in_=x_d[r0:r0 + 128, :]), writes=[rRX[s]], dma="xl")
        for db in range(8):
            banks = [(db % 2) * 4 + s for s in range(4)]
            for ku in range(4):
                slot, res = unit_tm(w_out_r, ku * 8, db * 512)
                uv = slot[:, :].rearrange("p (kc n) -> p kc n", kc=8)
                for s in range(4):
                    for kc in range(8):
                        k = ku * 8 + kc
                        S.op("pe", lambda E, s=s, kc=kc, k=k, uv=uv, b=banks[s]:
                             E.matmul(bank(b), MG[:, k, s * 128:(s + 1) * 128], uv[:, kc, :], start=(k == 0), stop=(k == 31)),
                             reads=[res, rMG], writes=[rPB[banks[s]]], mark=(kc == 7))
            for s in range(4):
                xp = X1[:, s, db * 512:(db + 1) * 512]
                S.op("dve", lambda E, xp=xp, b=banks[s]: E.scalar_tensor_tensor(xp, bank(b), 0.5, xp, ALU.mult, ALU.add),
                     reads=[rPB[banks[s]]], writes=[rXP[s][db]])

    def phase_mlp():
        hrhs = lambda k: HT[:, k, :]
        for g in range(8):
            for fb in range(8):
                units = [unit_fm(w_up_r, u * 16, g * 2048 + fb * 256) for u in range(2)]
                banks = [next_pa(), next_pa()]
                proj_fm(units, 32, hrhs, rR2, TT, banks)
                for j in range(2):
                    fc = fb * 2 + j
                    ti = fc % 2
                    r = TA[ti][:, :]
                    S.op("act", lambda E, r=r, b=banks[j]: E.activation(r, bank(b), AF.Relu), reads=[rPB[banks[j]]], writes=[rTA[ti]])
                    S.op("dve", lambda E, r=r, fc=fc: E.tensor_tensor(MG[:, fc, :], r, r, ALU.mult), reads=[rTA[ti]], writes=[rMGc[fc]])
            for db in range(8):
                banks = [4 + s for s in range(4)]
                for ku in range(2):
                    slot, res = unit_tm(w_dn_r, g * 16 + ku * 8, db * 512)
                    uv = slot[:, :].rearrange("p (kc n) -> p kc n", kc=8)
                    for s in range(4):
                        for kc in range(8):
                            k = ku * 8 + kc
                            S.op("pe", lambda E, s=s, kc=kc, k=k, uv=uv, b=banks[s]:
                                 E.matmul(bank(b), MG[:, k, s * 128:(s + 1) * 128], uv[:, kc, :], start=(k == 0), stop=(k == 15)),
                                 reads=[res, rMGc[k]], writes=[rPB[banks[s]]], mark=(kc == 7))
                for s in range(4):
                    xp = X1[:, s, db * 512:(db + 1) * 512]
                    S.op("dve", lambda E, xp=xp, b=banks[s]: E.tensor_tensor(xp, xp, bank(b), ALU.add),
                         reads=[rPB[banks[s]]], writes=[rXP[s][db]])

    def phase_final(t):
        ld(GFIN, gfin_d[:, :], rR2, key="gf")
        for s in range(4):
            ss = STAT[:, s:s + 1]
            sd = STAT[:, 8 + s:9 + s]
            rs = STAT[:, 16 + s:17 + s]
            S.op("act", lambda E, s=s, ss=ss: E.activation(XS[:, :], X1[:, s, :], AF.Square, accum_out=ss),
                 reads=[rRX[s]], writes=XSR + [rSS[s]])
            S.op("act", lambda E, ss=ss, sd=sd: E.activation(sd, ss, AF.Sqrt, bias=EPS_COL[:, 0:1], scale=1.0 / D),
                 reads=[rSS[s], rCONST], writes=[rSD[s]])
            S.op("dve", lambda E, sd=sd, rs=rs: E.reciprocal(rs, sd), reads=[rSD[s]], writes=[rRS[s]])
            S.op("dve", lambda E, s=s, rs=rs: E.scalar_tensor_tensor(X1[:, s, :], X1[:, s, :], rs, GFIN, ALU.mult, ALU.mult),
                 reads=[rRS[s], rR2], writes=[rRX[s]])
            r0 = t * TT + s * 128
            S.op("sp", lambda E, s=s, r0=r0: E.dma_start(out=y_d[r0:r0 + 128, :], in_=X1[:, s, :]), reads=[rRX[s]], dma="st")

    EPS_COL = STAT[:, 60:61]
    S.op("dve", lambda E: E.memset(EPS_COL, EPS), writes=[rCONST])

    import os as _os
    stop = dbg or ""
    if stop == "io":
        for t in range(NT):
            for s_ in range(4):
                r0 = t * TT + s_ * 128
                S.op("sp", lambda E, s_=s_, r0=r0: E.dma_start(out=X1[:, s_, :], in_=x_d[r0:r0 + 128, :]), writes=[rRX[s_]], dma="xl")
                S.op("sp", lambda E, s_=s_, r0=r0: E.dma_start(out=y_d[r0:r0 + 128, :], in_=X1[:, s_, :]), reads=[rRX[s_]], dma="st")
    if stop != "io":
        S.op("sp", lambda E: E.dma_start(out=X1[:, 0, :], in_=xh_d[:, :]), writes=[rRX[0]], dma="xl")
        ldm(COS[:, 0:128], cos_d[:, 0:128], CSR[0:2], key="cs")
        ldm(SIN[:, 0:128], sin_d[:, 0:128], CSR[2:4], key="cs")
        rms_to_fm(1, G1)
        if stop != "rms":
            phase_kv(128, 1, 0, 0)
        if stop not in ("rms", "kv"):
            for cp in range(8):
                conv_pair_proj(cp, 128, True)

    for t in range(NT if stop not in ("io", "rms", "kv", "conv") else 0):
        for s in range(4):
            r0 = t * TT + s * 128
            S.op("sp", lambda E, s=s, r0=r0: E.dma_start(out=X1[:, s, :], in_=x_d[r0:r0 + 128, :]), writes=[rRX[s]], dma="xl")
        ws_state["u"] = 0
        ws_state["mode"] = "cast"
        c0 = 128 + t * TT
        ldm(COS[:, :], cos_d[:, c0:c0 + TT], CSR[0:2], key="cs")
        ldm(SIN[:, :], sin_d[:, c0:c0 + TT], CSR[2:4], key="cs")
        rms_to_fm(4, G1)
        phase_kv(TT, 4, 128, 1)
        conv_pair_proj(0, TT, False)
        for _ in qproj_gen(0):
            pass
        for i in range(8):
            taps = conv_pair_taps_gen(i)
            fill = chain_gens(qproj_gen(i + 1), conv_pair_proj_gen(i + 1, TT, False)) if i < 7 else None
            quad_attn(i, t == 0, fill, taps)
            conv_pair_stats(i)
        merge_gates(0)
        phase_ln()
        if stop not in ("qa",):
            for s in range(2):
                r0 = t * TT + s * 128
                S.op("sp", lambda E, s=s, r0=r0: E.dma_start(out=X1[:, s, :], in_=x_d[r0:r0 + 128, :]), writes=[rRX[s]], dma="xl")
        if stop == "qa":
            break
        phase_merge()
        if stop == "merge":
            break
        phase_wout(t)
        if stop == "wout":
            break
        rms_to_fm(4, G2)
        phase_mlp()
        phase_final(t)
        if t + 1 < NT:
            S.op("act", lambda E: E.activation(KTv[:, :, 0:128], KTv[:, :, 512:640], AF.Identity), reads=[rKT], writes=[rKT])
            S.op("act", lambda E: E.activation(VD[:, 0:512], VD[:, 2048:2560], AF.Identity), reads=[rVD], writes=[rVD])

    if stop not in ("", "io"):
        for q_ in range(4):
            S.op("sp", lambda E, q_=q_: E.dma_start(out=y_d[q_ * 128:(q_ + 1) * 128, :], in_=X1[:, q_, :]),
                 reads=[rRX[q_]], dma="st")
    if S.cnt["st"] > 0:
        S.lists["sp"].append(lambda E: E.wait_ge(S.sems["st"], S.cnt["st"]))

    keys = list(ENGS) + sorted(S.dma_keys)
    for k in keys:
        S.sems[k] = es.enter_context(nc.semaphore(f"sem_{k}"))
    with nc.Block() as block:
        @block.tensor
        def _(E):
            for f in S.lists["pe"]:
                f(E)

        @block.scalar
        def _(E):
            for f in S.lists["act"]:
                f(E)

        @block.vector
        def _(E):
            for f in S.lists["dve"]:
                f(E)

        @block.gpsimd
        def _(E):
            for f in S.lists["pool"]:
                f(E)

        @block.sync
        def _(E):
            for f in S.lists["sp"]:
                f(E)
    es.close()
    return nc


def _host_consts(ntok_core, start_pos):
    half = 32
    inv_freq = (10000.0 ** (-np.arange(0, half, dtype=np.float32) / half)).astype(np.float32)
    pos = np.arange(start_pos - 128, start_pos + ntok_core, dtype=np.float32)
    ang = pos[None, :] * inv_freq[:, None]
    cos32 = np.cos(ang).astype(np.float32)
    sin32 = np.sin(ang).astype(np.float32)
    cos = np.concatenate([cos32, cos32, cos32, cos32], axis=0)
    sin = np.concatenate([-sin32, sin32, -sin32, sin32], axis=0)
    return np.ascontiguousarray(cos), np.ascontiguousarray(sin)


def _masks():
    q = np.arange(128)[:, None]
    kk = np.arange(256)[None, :]
    valid = (kk > q) & (kk <= q + 128)
    mn = np.where(valid, 0.0, MASKNEG).astype(np.float32)
    m0 = mn.copy()
    m0[:, :128] = MASKNEG
    return mn.astype(ml_dtypes.bfloat16), m0.astype(ml_dtypes.bfloat16)


_PROG_CACHE = {}
_DEBUG_HOOK = None
_DEBUG_MODE = None


def kernel(x, norm_mix_g, w_in, b_glu, w_dw, b_dw, conv_ln_g, conv_ln_b, w_conv_out,
           sinks, w_attn_out, w_out, norm_mlp_g, w_mlp_up, w_mlp_down, norm_final_g):
    x = np.asarray(x, dtype=np.float32)
    B, SEQ, _ = x.shape
    ncores = 8
    per_seq = ncores // B
    ntok = SEQ // per_seq
    NT = ntok // TT
    f32 = lambda a: np.ascontiguousarray(np.asarray(a, dtype=np.float32))

    if NT not in _PROG_CACHE:
        _PROG_CACHE[NT] = build_program(NT, dbg=_DEBUG_MODE)
    nc = _PROG_CACHE[NT]

    bf = ml_dtypes.bfloat16
    ident = np.eye(128, dtype=np.float32).astype(bf)
    perm = np.array([(m % 64 + 32) % 64 + 64 * (m // 64) for m in range(128)])
    rperm = np.zeros((128, 128), np.float32)
    rperm[perm, np.arange(128)] = 1.0
    rperm = rperm.astype(bf)
    ones = np.ones((128, 128), np.float32).astype(bf)
    maskn, mask0_first = _masks()

    pc = np.zeros((128, PC_N), np.float32)
    pc[:, PC_G1:PC_G1 + 32] = f32(norm_mix_g).reshape(32, 128).T
    pc[:, PC_G2:PC_G2 + 32] = f32(norm_mlp_g).reshape(32, 128).T
    pc[:, PC_BGLU:PC_BGLU + 32] = f32(b_glu).reshape(32, 128).T
    pc[:, PC_BDW:PC_BDW + 16] = f32(b_dw).reshape(16, 128).T
    pc[:, PC_LNG:PC_LNG + 16] = f32(conv_ln_g).reshape(16, 128).T
    pc[:, PC_LNB:PC_LNB + 16] = f32(conv_ln_b).reshape(16, 128).T
    pc[:, PC_SINK:PC_SINK + 32] = np.broadcast_to(f32(sinks)[None, :], (128, 32))
    wd = f32(w_dw).reshape(CW, 16, 128)
    pc[:, PC_WDW:PC_WDW + 496] = np.transpose(wd, (2, 1, 0)).reshape(128, 496)
    gfin = np.ascontiguousarray(np.broadcast_to(f32(norm_final_g)[None, :], (128, D)))

    shared = {
        "maskn": maskn, "ident": ident, "rperm": rperm, "ones": ones, "gfin": gfin,
        "w_in": f32(w_in), "w_conv_out": f32(w_conv_out), "w_attn_out": f32(w_attn_out),
        "w_out": f32(w_out), "w_mlp_up": f32(w_mlp_up), "w_mlp_down": f32(w_mlp_down),
    }
    _fake = bool(_DEBUG_MODE) and _DEBUG_MODE.endswith("_fake")
    need = _needed_weights(_DEBUG_MODE) if not _fake else ()
    for wn in ALLW:
        if wn not in need:
            shared[wn] = (np.random.RandomState(1).randn(4096, 512).astype(np.float32) / 64) if _fake else np.zeros((128, 128), np.float32)
    in_maps = []
    for c in range(ncores):
        b, hidx = c // per_seq, c % per_seq
        t0 = hidx * ntok
        xc = np.ascontiguousarray(x[b, t0:t0 + ntok, :])
        if hidx == 0:
            xh = np.zeros((128, D), np.float32)
        else:
            xh = np.ascontiguousarray(x[b, t0 - 128:t0, :])
        cos, sin = _host_consts(ntok, t0)
        pcc = pc.copy()
        pcc[:, PC_FLAG] = 0.0 if hidx == 0 else 1.0
        m = dict(shared)
        m.update({"x": xc, "xh": xh, "cos_t": cos, "sin_t": sin,
                  "mask0": mask0_first if hidx == 0 else maskn, "pcols": pcc})
        in_maps.append(m)
    if _DEBUG_HOOK is not None:
        return _DEBUG_HOOK(nc, in_maps)
    res = run_bass_kernel_spmd(nc, in_maps, core_ids=list(range(ncores)))
    out = np.empty((B, SEQ, D), np.float32)
    for c in range(ncores):
        b, hidx = c // per_seq, c % per_seq
        out[b, hidx * ntok:(hidx + 1) * ntok, :] = np.asarray(res.results[c]["y"], dtype=np.float32)
    return out
```

```python
import numpy as np
import ml_dtypes
from contextlib import ExitStack
from collections import defaultdict

import concourse.bass as bass
import concourse.mybir as mybir
from concourse.bass_utils import run_bass_kernel_spmd

F32 = mybir.dt.float32
BF16 = mybir.dt.bfloat16
AF = mybir.ActivationFunctionType
ALU = mybir.AluOpType
AX = mybir.AxisListType

D = 4096
NQH = 32
OFF_Q, OFF_K, OFF_V, OFF_CONV, OFF_GC, OFF_GA = 0, 2048, 2304, 2560, 6656, 10752
IN_W = 14848
DFF = 16384
CW = 31
EPS = 1e-6
TT = 512
NSLOT = 4
MASKNEG = -30000.0

PC_G1, PC_G2, PC_BGLU, PC_BDW, PC_LNG, PC_LNB, PC_SINK, PC_WDW, PC_FLAG, PC_N = 0, 32, 64, 96, 112, 128, 144, 176, 672, 673


ALLW = ("w_in", "w_conv_out", "w_attn_out", "w_out", "w_mlp_up", "w_mlp_down")


def _needed_weights(dbg):
    if dbg in ("io", "rms"):
        return ()
    if dbg in ("kv", "conv", "ln", "qa"):
        return ("w_in",)
    if dbg == "merge":
        return ("w_in", "w_conv_out", "w_attn_out")
    if dbg == "wout":
        return ("w_in", "w_conv_out", "w_attn_out", "w_out")
    return ALLW


class Res:
    __slots__ = ("name", "w", "r", "parent", "children")

    def __init__(self, name, parent=None):
        self.name = name
        self.w = None
        self.r = []
        self.parent = parent
        self.children = []
        if parent is not None:
            parent.children.append(self)


ENGS = ("pe", "act", "dve", "pool", "sp")


class Sched:
    def __init__(self):
        self.lists = {e: [] for e in ENGS}
        self.cnt = defaultdict(int)
        self.seen = {e: defaultdict(int) for e in ENGS}
        self.sems = {}
        self.dma_keys = set()

    def _wait(self, eng, ev):
        key, val = ev
        if key in self.dma_keys:
            val = self.cnt[key]
        if self.seen[eng][key] >= val:
            return
        self.seen[eng][key] = val
        self.lists[eng].append(lambda E, key=key, val=val: E.wait_ge(self.sems[key], val))

    def op(self, eng, fn, reads=(), writes=(), mark=True, dma=None, nowait=False):
        excl = [r for r in reads if r.name.startswith("PB")]
        if excl:
            reads = [r for r in reads if not r.name.startswith("PB")]
            writes = list(writes) + excl
        deps = []
        for r in reads:
            for q in (r, r.parent):
                if q is not None and q.w is not None:
                    deps.append((q.w, "raw"))
            for c in r.children:
                if c.w is not None:
                    deps.append((c.w, "raw"))
        for w in writes:
            fam = [w] + ([w.parent] if w.parent is not None else []) + list(w.children)
            for q in fam:
                if q.w is not None:
                    deps.append((q.w, "waw"))
                for ev in q.r:
                    deps.append((ev, "war"))
        if nowait:
            deps = []
        for ev, kind in deps:
            if ev[0] == eng and eng == "pe":
                continue
            self._wait(eng, ev)
        if dma is not None:
            self.dma_keys.add(dma)
            self.cnt[dma] += 16
            ev = (dma, self.cnt[dma])
            self.lists[eng].append(lambda E, fn=fn, dma=dma: fn(E).then_inc(self.sems[dma], 16))
        elif mark:
            self.cnt[eng] += 1
            ev = (eng, self.cnt[eng])
            self.lists[eng].append(lambda E, fn=fn, eng=eng: fn(E).then_inc(self.sems[eng], 1))
        else:
            ev = (eng, self.cnt[eng] + 1)
            self.lists[eng].append(lambda E, fn=fn: fn(E))
        for r in reads:
            r.r.append(ev)
            if len(r.r) > 24:
                r.r = r.r[-24:] if False else self._compact(r.r)
        for w in writes:
            w.w = ev
            w.r = []

    @staticmethod
    def _compact(evs):
        best = {}
        for k, v in evs:
            if v > best.get(k, -1):
                best[k] = v
        return list(best.items())


def build_program(NT=4, dbg=None):
    NTOK = NT * TT
    nc = bass.Bass("TRN2", target_bir_lowering=False)

    def din(name, shape, dt=F32):
        return nc.dram_tensor(name, shape, dt, kind="ExternalInput").ap()

    x_d = din("x", [NTOK, D])
    xh_d = din("xh", [128, D])
    cos_d = din("cos_t", [128, 128 + NTOK])
    sin_d = din("sin_t", [128, 128 + NTOK])
    mask0_d = din("mask0", [128, 256], BF16)
    maskn_d = din("maskn", [128, 256], BF16)
    ident_d = din("ident", [128, 128], BF16)
    rperm_d = din("rperm", [128, 128], BF16)
    ones_d = din("ones", [128, 128], BF16)
    pcols_d = din("pcols", [128, PC_N])
    gfin_d = din("gfin", [128, D])
    fake = bool(dbg) and dbg.endswith("_fake")
    if fake:
        dbg = dbg[:-5]
    need = _needed_weights(dbg) if not fake else ()
    FSH = [4096, 512]
    KVL = 9
    import os
    KVFIX = os.environ.get('KVFIX') == '1'
    if dbg and dbg.startswith("kv") and len(dbg) == 3:
        KVL = int(dbg[2])
        dbg = "kv"
    w_in_d = din("w_in", [D, IN_W] if "w_in" in need else (FSH if fake else [128, 128]))
    w_co_d = din("w_conv_out", [2048, D] if "w_conv_out" in need else (FSH if fake else [128, 128]))
    w_ao_d = din("w_attn_out", [2048, D] if "w_attn_out" in need else (FSH if fake else [128, 128]))
    w_out_d = din("w_out", [D, D] if "w_out" in need else (FSH if fake else [128, 128]))
    w_up_d = din("w_mlp_up", [D, DFF] if "w_mlp_up" in need else (FSH if fake else [128, 128]))
    w_dn_d = din("w_mlp_down", [DFF, D] if "w_mlp_down" in need else (FSH if fake else [128, 128]))
    y_d = nc.dram_tensor("y", [NTOK, D], F32, kind="ExternalOutput").ap()
    NUMAX = 0
    UPS = 110
    wscr_parts = [nc.dram_tensor(f"wscr{k}", [UPS * 128, 4096], BF16).ap() for k in range(NUMAX // UPS)]

    def wscr_unit(u):
        return wscr_parts[u // UPS][(u % UPS) * 128:(u % UPS + 1) * 128, :]

    w_in_r = w_in_d.rearrange("(kc p) n -> p kc n", p=128)
    w_co_r = w_co_d.rearrange("(kc p) n -> p kc n", p=128)
    w_ao_r = w_ao_d.rearrange("(kc p) n -> p kc n", p=128)
    w_out_r = w_out_d.rearrange("(kc p) n -> p kc n", p=128)
    w_up_r = w_up_d.rearrange("(kc p) n -> p kc n", p=128)
    w_dn_r = w_dn_d.rearrange("(kc p) n -> p kc n", p=128)

    class _FakeView:
        def __init__(self, r):
            self.r = r

        def __getitem__(self, key):
            p, kc, cc = key
            nk = kc.stop - kc.start
            ncol = cc.stop - cc.start
            k0 = kc.start % (32 - nk + 1)
            c0 = cc.start % (512 - ncol + 1)
            return self.r[p, k0:k0 + nk, c0:c0 + ncol]

    if fake:
        w_in_r, w_co_r, w_ao_r, w_out_r, w_up_r, w_dn_r = [_FakeView(r) for r in (w_in_r, w_co_r, w_ao_r, w_out_r, w_up_r, w_dn_r)]

    S = Sched()
    es = ExitStack()

    def sb(name, shape, dt):
        return es.enter_context(nc.sbuf_tensor(name, shape, dt))

    R1 = sb("R1", [128, 4 * D], F32)
    R2 = sb("R2", [128, 32 * TT], BF16)
    R3 = sb("R3", [128, 32 * TT], BF16)
    WS = [sb(f"ws{i}", [128, 4096], BF16) for i in range(NSLOT)]
    KT = sb("KT", [128, 4 * 640], BF16)
    VD = sb("VD", [128, 5 * 512], BF16)
    QT = sb("QT", [128, 2 * 2 * TT], BF16)
    IDENT = sb("IDENT", [128, 128], BF16)
    RPERM = sb("RPERM", [128, 128], BF16)
    ONES = sb("ONES", [128, 128], BF16)
    MASKN = sb("MASKN", [128, 256], BF16)
    MASK0 = sb("MASK0", [128, 256], BF16)
    PCOLS = sb("PCOLS", [128, PC_N], F32)
    DCOLS = sb("DCOLS", [128, 80], F32)
    HB = sb("HB", [128, 16 * 30], F32)
    STAT = sb("STAT", [128, 64], F32)
    GLU = [sb(f"GLU{i}", [128, 30 + TT], F32) for i in range(2)]
    TA = [sb(f"TA{i}", [128, TT], F32) for i in range(2)]
    _tb = sb("TB0", [128, TT], F32)
    TB = [_tb, _tb]
    TC = [sb(f"TC{i}", [128, 2 * TT], BF16) for i in range(2)]
    E4 = [sb(f"E4{i}", [128, 1024], BF16) for i in range(2)]
    PT = [sb(f"PT{i}", [128, 1024], BF16) for i in range(2)]
    PS = es.enter_context(nc.psum_tensor("PS", [128, 8 * 512], F32))

    X1 = R1[:, :].rearrange("p (s d) -> p s d", s=4)
    DW = R1[:, 0:8192].rearrange("p (c n) -> p c n", c=16)
    ZT = R1[:, 8192:12288].bitcast(BF16).rearrange("p (c n) -> p c n", c=16)
    AT = R1[:, 12288:16384].bitcast(BF16).rearrange("p (c n) -> p c n", c=16)
    HT = R2[:, :].rearrange("p (c n) -> p c n", c=32)
    GFIN = R2[:, 0:8192].bitcast(F32)
    MG = R3[:, :].rearrange("p (c n) -> p c n", c=32)
    KTv = KT[:, :].rearrange("p (g n) -> p g n", g=4)
    VDv = VD[:, :].rearrange("p (b g o d) -> p b g o d", b=5, g=4, o=2)
    QTv = QT[:, :].rearrange("p (s c n) -> p s c n", s=2, c=2)
    HBv = HB[:, :].rearrange("p (c n) -> p c n", c=16)
    WDW = PCOLS[:, PC_WDW:PC_WDW + 496]
    XS = R3[:, 24 * TT:32 * TT]
    COS = R3[:, 16 * TT:18 * TT].bitcast(F32)
    SIN = R3[:, 18 * TT:20 * TT].bitcast(F32)
    MEAN = R3[:, 20 * TT:22 * TT].bitcast(F32)
    RSTD = R3[:, 22 * TT:24 * TT].bitcast(F32)

    def bank(b, n=1):
        return PS[:, b * 512:(b + n) * 512]

    def bank_bf(b):
        return PS[:, b * 512:(b + 1) * 512].bitcast(BF16)

    rRX = [Res(f"RX{s}") for s in range(4)]
    rDWh = [[Res(f"DW{c}_{h}", rRX[c // 8]) for h in range(2)] for c in range(16)]
    rZT = [Res(f"ZT{c}", rRX[2]) for c in range(16)]
    rAT = Res("AT", rRX[3])
    rXP = [[Res(f"XP{s}_{d}", rRX[s]) for d in range(8)] for s in range(4)]
    rR2 = Res("R2")
    rMG = Res("MG")
    rMGc = [Res(f"MG{m}", rMG) for m in range(32)]
    rWS = [Res(f"WS{i}") for i in range(NSLOT)]
    rKT = Res("KT")
    rVD = Res("VD")
    rQT = [Res("QT0"), Res("QT1")]
    rCONST = Res("CONST")
    rDC = Res("DCOLS")
    rHB = [Res(f"HB{c}") for c in range(16)]
    rGLU = [Res("GLU0"), Res("GLU1")]
    rTA = [Res("TA0"), Res("TA1")]
    _rtb = Res("TB0")
    rTB = [_rtb, _rtb]
    rTC = [Res("TC0"), Res("TC1")]
    rE4 = [Res("E40"), Res("E41")]
    rPT = [Res("PT0"), Res("PT1")]
    rPB = [Res(f"PB{b}") for b in range(8)]
    rMEAN = rMGc[20:22]
    rRSTD = rMGc[22:24]
    XSR = rMGc[24:32]
    XSB = [XS, R3[:, 8 * TT:16 * TT]]
    XSBR = [rMGc[24:32], rMGc[8:16]]
    JUNK = R3[:, 0:8 * TT]
    JUNKR = rMGc[0:8]
    rSS = [Res(f"SS{i}") for i in range(4)]
    rSD = [Res(f"SD{i}") for i in range(4)]
    rRS = [Res(f"RS{i}") for i in range(4)]
    CSR = rMGc[16:20]
    rST = [Res(f"ST{i}") for i in range(8)]
    rMX = [Res("MX0"), Res("MX1")]
    rNB = [Res("NB0"), Res("NB1")]
    rRSUM = [Res("RS0"), Res("RS1")]
    rSK = [Res("SK0"), Res("SK1")]

    def ld(dst_ap, src_ap, res, key="ld", eng="sp"):
        S.op(eng, lambda E: E.dma_start(out=dst_ap, in_=src_ap), writes=[res], dma=key)

    def ldm(dst_ap, src_ap, ress, key="ld", eng="sp"):
        S.op(eng, lambda E: E.dma_start(out=dst_ap, in_=src_ap), writes=list(ress), dma=key)

    for dst, src in ((IDENT, ident_d), (RPERM, rperm_d), (ONES, ones_d), (MASKN, maskn_d),
                     (MASK0, mask0_d), (PCOLS, pcols_d)):
        ld(dst[:, :], src[:, :], rCONST)
    S.op("dve", lambda E: E.tensor_scalar(DCOLS[:, 0:32], PCOLS[:, PC_BGLU:PC_BGLU + 32], 0.5, None, ALU.mult),
         reads=[rCONST], writes=[rDC])
    S.op("dve", lambda E: E.tensor_scalar(DCOLS[:, 32:64], PCOLS[:, PC_LNG:PC_LNG + 32], 0.5, None, ALU.mult),
         reads=[rCONST], writes=[rDC])
    S.op("dve", lambda E: E.reduce_max(DCOLS[:, 65:66], PCOLS[:, PC_SINK:PC_SINK + 32], AX.X),
         reads=[rCONST], writes=[rDC])
    S.op("dve", lambda E: E.tensor_scalar(DCOLS[:, 64:65], DCOLS[:, 65:66], -1.0, None, ALU.mult),
         reads=[rDC], writes=[rDC])
    HBGLU = DCOLS[:, 0:32]
    HLNG = DCOLS[:, 32:48]
    HLNB = DCOLS[:, 48:64]
    NSM = DCOLS[:, 64:65]
    G1 = PCOLS[:, PC_G1:PC_G1 + 32]
    G2 = PCOLS[:, PC_G2:PC_G2 + 32]
    BDW = PCOLS[:, PC_BDW:PC_BDW + 16]
    SINK = PCOLS[:, PC_SINK:PC_SINK + 32]
    FLAG = PCOLS[:, PC_FLAG:PC_FLAG + 1]

    ws_state = {"i": 0, "mode": "cast", "u": 0}
    rSCR = [Res(f"SCR{u}") for u in range(NUMAX)]

    def load_unit(pieces):
        i = ws_state["i"] % NSLOT
        ws_state["i"] += 1
        slot, res = WS[i], rWS[i]
        mode = ws_state["mode"]
        u = ws_state["u"]
        if mode != "cast":
            ws_state["u"] += 1
            assert u < NUMAX
        if mode == "consume":
            S.op("pool", lambda E, u=u: E.dma_start(out=slot[:, :], in_=wscr_unit(u)),
                 reads=[rSCR[u]], writes=[res], dma=f"w{i}")
            return slot, res
        for pi, (dst_fn, src) in enumerate(pieces):
            dst = dst_fn(slot)
            S.op("pool", lambda E, dst=dst, src=src: E.dma_start(out=dst, in_=src), writes=[res], dma=f"w{i}",
                 nowait=(pi > 0))
        if mode == "produce":
            S.op("sp", lambda E, u=u: E.dma_start(out=wscr_unit(u), in_=slot[:, :]),
                 reads=[res], writes=[rSCR[u]], dma="ws")
        return slot, res

    def unit_fm(src_r, kc0, c0):
        return load_unit([(lambda s: s[:, :].rearrange("p (kc n) -> p kc n", kc=16),
                           src_r[:, kc0:kc0 + 16, c0:c0 + 256])])

    def unit_tm(src_r, kc0, c0):
        return load_unit([(lambda s: s[:, :].rearrange("p (kc n) -> p kc n", kc=8),
                           src_r[:, kc0:kc0 + 8, c0:c0 + 512])])

    pa_state = {"i": 0}

    def next_pa():
        b = pa_state["i"] % 4
        pa_state["i"] += 1
        return b

    def proj_fm_gen(unit_fns, nk, rhs_fn, rhs_res, ntok, banks):
        nu = len(unit_fns)
        for u, ufn in enumerate(unit_fns):
            slot, res = ufn()
            uv = slot[:, :].rearrange("p (kc n) -> p kc n", kc=16)
            for j in range(2):
                for kc in range(16):
                    k = u * 16 + kc
                    last = (u == nu - 1 and kc == 15)
                    S.op("pe", lambda E, j=j, kc=kc, k=k, uv=uv, last=last:
                         E.matmul(bank(banks[j])[:, 0:ntok], uv[:, kc, j * 128:(j + 1) * 128], rhs_fn(k),
                                  start=(k == 0), stop=last),
                         reads=[res, rhs_res], writes=[rPB[banks[j]]],
                         mark=(last or (j == 1 and kc == 15)))
                yield

    def proj_fm(units, nk, rhs_fn, rhs_res, ntok, banks):
        for _ in proj_fm_gen([(lambda x=x: x) for x in units], nk, rhs_fn, rhs_res, ntok, banks):
            pass

    def rms_to_fm(nsub, gcols, tokoff_fn=None):
        def st1(s):
            ss = STAT[:, s:s + 1]
            sd = STAT[:, 8 + s:9 + s]
            rs = STAT[:, 16 + s:17 + s]
            xs, xr = XSB[s % 2], XSBR[s % 2]
            S.op("act", lambda E, s=s, ss=ss: E.activation(JUNK[:, :], X1[:, s, :], AF.Square, accum_out=ss),
                 reads=[rRX[s]], writes=JUNKR + [rSS[s]])
            S.op("act", lambda E, ss=ss, sd=sd: E.activation(sd, ss, AF.Sqrt, bias=EPS_COL[:, 0:1], scale=1.0 / D),
                 reads=[rSS[s], rCONST], writes=[rSD[s]])
            S.op("dve", lambda E, sd=sd, rs=rs: E.reciprocal(rs, sd), reads=[rSD[s]], writes=[rRS[s]])
            S.op("dve", lambda E, s=s, rs=rs, xs=xs: E.tensor_scalar(xs[:, :], X1[:, s, :], rs, None, ALU.mult),
                 reads=[rRX[s], rRS[s]], writes=xr)

        def st2(s):
            xs, xr = XSB[s % 2], XSBR[s % 2]
            for cg in range(4):
                b = next_pa()
                pv = bank_bf(b).rearrange("p (c n) -> p c n", c=8)
                for ci in range(8):
                    c = cg * 8 + ci
                    S.op("pe", lambda E, pv=pv, ci=ci, c=c, xs=xs: E.transpose(pv[:, ci, :], xs[:, c * 128:(c + 1) * 128], IDENT[:, :]),
                         reads=xr + [rCONST], writes=[rPB[b]], mark=(ci == 7))
                for ci in range(8):
                    c = cg * 8 + ci
                    S.op("act", lambda E, pv=pv, ci=ci, c=c, s=s:
                         E.activation(HT[:, c, s * 128:(s + 1) * 128], pv[:, ci, :], AF.Identity, scale=gcols[:, c:c + 1]),
                         reads=[rPB[b], rCONST], writes=[rR2])

        st1(0)
        for s in range(nsub):
            if s + 1 < nsub:
                st1(s + 1)
            st2(s)

    def rope(b_raw, dst_ap, ntok, dst_res, csoff, b_rot=2):
        ti = rope_state["i"] % 2
        rope_state["i"] += 1
        qb = TC[ti][:, 0:ntok]
        t1 = TA[ti][:, 0:ntok]
        t2 = TB[ti][:, 0:ntok]
        raw = bank(b_raw)[:, 0:ntok]
        rot = bank(b_rot)[:, 0:ntok]
        S.op("act", lambda E: E.activation(qb, raw, AF.Identity), reads=[rPB[b_raw]], writes=[rTC[ti]])
        S.op("pe", lambda E: E.matmul(rot, RPERM[:, :], qb, start=True, stop=True),
             reads=[rTC[ti], rCONST], writes=[rPB[b_rot]])
        S.op("dve", lambda E: E.tensor_tensor(t1, raw, COS[:, csoff:csoff + ntok], ALU.mult),
             reads=[rPB[b_raw]] + CSR, writes=[rTA[ti]])
        S.op("dve", lambda E: E.tensor_tensor(t2, rot, SIN[:, csoff:csoff + ntok], ALU.mult),
             reads=[rPB[b_rot]] + CSR, writes=[rTB[ti]])
        S.op("dve", lambda E: E.tensor_tensor(dst_ap, t1, t2, ALU.add),
             reads=[rTA[ti], rTB[ti]], writes=[dst_res])

    rope_state = {"i": 0}

    def phase_kv(ntok, nsub, koff, vb0):
        rhs_fn = lambda k: HT[:, k, 0:ntok]
        for gp in range(2):
            units = []
            for u in range(2):
                pieces = []
                for o in range(2):
                    for gg in range(2):
                        c0 = OFF_K + gp * 128 + gg * 64
                        pieces.append((lambda s, o=o, gg=gg: s[:, :].rearrange("p (kc g o d) -> p kc g o d", kc=16, g=2, o=2)[:, :, gg, o, :],
                                       w_in_r[:, u * 16:(u + 1) * 16, c0:c0 + 64]))
                units.append(load_unit(pieces))
            if KVL == 1:
                return
            banks = [next_pa(), next_pa()] if not KVFIX else [0, 1]
            proj_fm(units, 32, rhs_fn, rR2, ntok, banks)
            if KVL == 2:
                return
            for j in range(2):
                g = gp * 2 + j
                if KVL == 7 or (KVL == 8 and gp == 0):
                    continue
                rope(banks[j], KTv[:, g, koff:koff + ntok], ntok, rKT, 0, b_rot=5)
            if KVL == 3:
                return
        if KVL in (4, 7, 8):
            return
        units = [unit_fm(w_in_r, u * 16, OFF_V) for u in range(2)]
        vbanks = [next_pa() for _ in range(nsub)]
        for u, (slot, res) in enumerate(units):
            uv = slot[:, :].rearrange("p (kc n) -> p kc n", kc=16)
            for blk in range(nsub):
                for kc in range(16):
                    k = u * 16 + kc
                    S.op("pe", lambda E, blk=blk, kc=kc, k=k, uv=uv:
                         E.matmul(bank(vbanks[blk])[:, 0:256], HT[:, k, blk * 128:(blk + 1) * 128], uv[:, kc, :],
                                  start=(k == 0), stop=(k == 31)),
                         reads=[res, rR2], writes=[rPB[vbanks[blk]]], mark=(kc == 15))
        if KVL == 5:
            return
        for blk in range(nsub):
            src = bank(vbanks[blk])[:, 0:256].rearrange("p (g d) -> p g d", g=4)
            for o in range(2):
                S.op("act", lambda E, blk=blk, o=o, src=src: E.activation(VDv[:, vb0 + blk, :, o, :], src, AF.Identity),
                     reads=[rPB[vbanks[blk]]], writes=[rVD])

    def conv_pair_proj_gen(cp, ntok, halo):
        rhs_fn = lambda k: HT[:, k, 0:ntok]
        for gi in range(2):
            c = cp * 2 + gi
            unit_fns = []
            for u in range(2):
                unit_fns.append(lambda u=u, c=c: load_unit([(
                    lambda s, two=two: s[:, :].rearrange("p (kc two n) -> p kc two n", kc=16, two=2)[:, :, two, :],
                    w_in_r[:, u * 16:(u + 1) * 16, OFF_CONV + two * 2048 + c * 128:OFF_CONV + two * 2048 + (c + 1) * 128])
                    for two in range(2)]))
            banks = [6, 7] if gi == 0 else [0, 1]
            yield from proj_fm_gen(unit_fns, 32, rhs_fn, rR2, ntok, banks)
            glu = GLU[gi]
            th = TA[gi][:, 0:ntok]
            S.op("act", lambda E, th=th, c=c, b=banks[1]: E.activation(th, bank(b)[:, 0:ntok], AF.Tanh,
                                                                      bias=HBGLU[:, 16 + c:17 + c], scale=0.5),
                 reads=[rPB[banks[1]], rDC], writes=[rTA[gi]])
            S.op("act", lambda E, glu=glu, c=c, b=banks[0]: E.activation(glu[:, 30:30 + ntok], bank(b)[:, 0:ntok], AF.Identity,
                                                                        bias=HBGLU[:, c:c + 1], scale=0.5),
                 reads=[rPB[banks[0]], rDC], writes=[rGLU[gi]])
            if not halo:
                S.op("act", lambda E, glu=glu, c=c: E.activation(glu[:, 0:30], HBv[:, c, :], AF.Identity),
                     reads=[rHB[c]], writes=[rGLU[gi]])
            S.op("dve", lambda E, glu=glu, th=th: E.scalar_tensor_tensor(glu[:, 30:30 + ntok], th, 1.0, glu[:, 30:30 + ntok], ALU.add, ALU.mult),
                 reads=[rTA[gi], rGLU[gi]], writes=[rGLU[gi]])
            if halo:
                S.op("dve", lambda E, glu=glu, c=c: E.tensor_scalar(HBv[:, c, :], glu[:, ntok:ntok + 30], FLAG, None, ALU.mult),
                     reads=[rGLU[gi], rCONST], writes=[rHB[c]])
            else:
                S.op("act", lambda E, glu=glu, c=c: E.activation(HBv[:, c, :], glu[:, ntok:ntok + 30], AF.Identity),
                     reads=[rGLU[gi]], writes=[rHB[c]])

    def conv_pair_proj(cp, ntok, halo):
        for _ in conv_pair_proj_gen(cp, ntok, halo):
            pass

    def conv_pair_taps_gen(cp):
        for j in range(CW):
            if j > 0:
                yield
            for gi in range(2):
                c = cp * 2 + gi
                glu = GLU[gi]
                o = DW[:, c, :]
                g_in = glu[:, j:j + TT]
                wcol = WDW[:, c * 31 + j:c * 31 + j + 1]
                if j == 0:
                    S.op("dve", lambda E, o=o, g_in=g_in, wcol=wcol, c=c:
                         E.tensor_scalar(o, g_in, wcol, BDW[:, c:c + 1], ALU.mult, ALU.add),
                         reads=[rGLU[gi], rCONST], writes=rDWh[c])
                else:
                    S.op("dve", lambda E, o=o, g_in=g_in, wcol=wcol:
                         E.scalar_tensor_tensor(o, g_in, wcol, o, ALU.mult, ALU.add),
                         reads=[rGLU[gi], rCONST] + rDWh[c], writes=rDWh[c])

    def conv_pair_stats(cp):
        for gi in range(2):
            c = cp * 2 + gi
            dwb = TC[gi][:, 0:TT]
            sq = TC[gi][:, TT:2 * TT]
            S.op("act", lambda E, dwb=dwb, c=c: E.activation(dwb, DW[:, c, :], AF.Identity), reads=rDWh[c], writes=[rTC[gi]])
            S.op("act", lambda E, sq=sq, c=c: E.activation(sq, DW[:, c, :], AF.Square), reads=rDWh[c], writes=[rTC[gi]])
        for which in range(2):
            for gi in range(2):
                src = TC[gi][:, which * TT:(which + 1) * TT]
                S.op("pe", lambda E, src=src, which=which, gi=gi: E.matmul(bank(which), ONES[:, :], src, start=(gi == 0), stop=(gi == 1)),
                     reads=[rTC[gi], rCONST], writes=[rPB[which]], mark=(gi == 1))
        for which, (acc, racc) in enumerate(((MEAN, rMEAN), (RSTD, rRSTD))):
            if cp == 0:
                S.op("dve", lambda E, acc=acc, which=which: E.tensor_copy(acc[:, :], bank(which)), reads=[rPB[which]], writes=racc)
            else:
                S.op("dve", lambda E, acc=acc, which=which: E.tensor_tensor(acc[:, :], acc[:, :], bank(which), ALU.add),
                     reads=[rPB[which]] + racc, writes=racc)

    def phase_ln():
        msq = TA[0][:, :]
        S.op("dve", lambda E: E.tensor_scalar(MEAN[:, :], MEAN[:, :], 1.0 / 2048, None, ALU.mult), reads=rMEAN, writes=rMEAN)
        S.op("dve", lambda E: E.tensor_tensor(msq, MEAN[:, :], MEAN[:, :], ALU.mult), reads=rMEAN, writes=[rTA[0]])
        S.op("dve", lambda E: E.scalar_tensor_tensor(RSTD[:, :], RSTD[:, :], 1.0 / 2048, msq, ALU.mult, ALU.subtract),
             reads=rRSTD + [rTA[0]], writes=rRSTD)
        S.op("act", lambda E: E.activation(RSTD[:, :], RSTD[:, :], AF.Sqrt, bias=EPS_COL[:, 0:1], scale=1.0),
             reads=rRSTD + [rCONST], writes=rRSTD)
        S.op("dve", lambda E: E.reciprocal(RSTD[:, :], RSTD[:, :]), reads=rRSTD, writes=rRSTD)
        THB = [PT[0][:, :].bitcast(F32), PT[1][:, :].bitcast(F32)]
        YHB = [GLU[0][:, 0:TT], GLU[1][:, 0:TT]]

        def ln_a(c):
            i = c % 2
            t, th, yh = TA[i][:, :], THB[i], YHB[i]
            S.op("dve", lambda E, t=t, c=c: E.tensor_tensor(t, DW[:, c, :], MEAN[:, :], ALU.subtract),
                 reads=rDWh[c] + rMEAN, writes=[rTA[i]])
            S.op("dve", lambda E, t=t: E.tensor_tensor(t, t, RSTD[:, :], ALU.mult), reads=[rTA[i]] + rRSTD, writes=[rTA[i]])
            S.op("act", lambda E, t=t, th=th, c=c: E.activation(th, t, AF.Tanh, bias=HLNB[:, c:c + 1], scale=HLNG[:, c:c + 1]),
                 reads=[rTA[i], rDC], writes=[rPT[i]])
            S.op("dve", lambda E, t=t, yh=yh, c=c: E.tensor_scalar(yh, t, HLNG[:, c:c + 1], HLNB[:, c:c + 1], ALU.mult, ALU.add),
                 reads=[rTA[i], rDC], writes=[rGLU[i]])

        def ln_b(c):
            i = c % 2
            th, yh = THB[i], YHB[i]
            S.op("dve", lambda E, th=th, yh=yh, c=c: E.scalar_tensor_tensor(ZT[:, c, :], th, 1.0, yh, ALU.add, ALU.mult),
                 reads=[rGLU[i], rPT[i]], writes=[rZT[c]])

        ln_a(0)
        for c in range(16):
            if c + 1 < 16:
                ln_a(c + 1)
            ln_b(c)

    qa_k = {"k": 0}

    def qproj_gen(hq):
        slot = hq % 2
        unit_fns = [(lambda u=u: unit_fm(w_in_r, u * 16, OFF_Q + hq * 256)) for u in range(2)]
        banks = [0, 1]
        yield from proj_fm_gen(unit_fns, 32, lambda k: HT[:, k, :], rR2, TT, banks)
        for j in range(2):
            rope(banks[j], QTv[:, slot, j, :], TT, rQT[slot], 0, b_rot=6 + j)

    def chain_gens(*gens):
        for g in gens:
            if g is not None:
                yield from g

    def pull_gen(g, n):
        if g is None:
            return
        for _ in range(n):
            try:
                next(g)
            except StopIteration:
                return

    def quad_attn(hq, first_tile, filler=None, dfiller=None):
        rhs_fn = lambda k: HT[:, k, :]
        slot_idx = [0]

        def slot_fill():
            k = slot_idx[0]
            slot_idx[0] += 1
            if k == 4:
                pull_gen(dfiller, 1000)
            pull(2)

        def dfill(n=2):
            pull_gen(dfiller, n)

        def pull(n=1):
            if filler is None:
                return
            for _ in range(n):
                try:
                    next(filler)
                except StopIteration:
                    return

        def qproj(hq):
            slot = hq % 2
            units = [unit_fm(w_in_r, u * 16, OFF_Q + hq * 256) for u in range(2)]
            banks = [0, 1]
            proj_fm(units, 32, rhs_fn, rR2, TT, banks)
            for j in range(2):
                rope(banks[j], QTv[:, slot, j, :], TT, rQT[slot], 0)

        def stage_a(hq, n, k):
            slot = hq % 2
            g = hq // 2
            sb_ = 3
            e4 = E4[k % 2]
            mask = MASK0 if (first_tile and n == 0) else MASKN
            s4 = bank(sb_, 2)
            s4v = s4.rearrange("p (i n) -> p i n", i=4)
            for i in range(4):
                hf, ci = i // 2, i % 2
                p0 = hf * 64
                S.op("pe", lambda E, i=i, ci=ci, p0=p0, s4v=s4v:
                     E.matmul(s4v[:, i, :], QTv[p0:p0 + 64, slot, ci, n * 128:(n + 1) * 128],
                              KTv[p0:p0 + 64, g, n * 128:n * 128 + 256], start=True, stop=False),
                     reads=[rQT[slot], rKT], writes=[rPB[sb_], rPB[sb_ + 1]], mark=False)
                S.op("pe", lambda E, i=i, s4v=s4v, mask=mask:
                     E.matmul(s4v[:, i, :], IDENT[:, :], mask[:, :], start=False, stop=True),
                     reads=[rCONST], writes=[rPB[sb_], rPB[sb_ + 1]], mark=(i == 3))
            mx = STAT[:, 24 + (k % 2):25 + (k % 2)]
            nb = STAT[:, 26 + (k % 2):27 + (k % 2)]
            rsum = STAT[:, 32 + 4 * (k % 2):36 + 4 * (k % 2)]
            sk = STAT[:, 40 + 4 * (k % 2):44 + 4 * (k % 2)]
            rmx = rMX[k % 2]
            rnb = rNB[k % 2]
            rrs = rRSUM[k % 2]
            rsk = rSK[k % 2]
            S.op("dve", lambda E: E.reduce_max(mx, s4, AX.X), reads=[rPB[sb_], rPB[sb_ + 1]], writes=[rmx])
            S.op("dve", lambda E: E.tensor_scalar(nb, mx, -0.125, NSM, ALU.mult, ALU.min), reads=[rmx, rDC], writes=[rnb])
            dfill(4)
            e4v = e4[:, :].rearrange("p (i n) -> p i n", i=4)
            for i in range(4):
                S.op("act", lambda E, i=i: E.activation(e4v[:, i, :], s4v[:, i, :], AF.Exp, bias=nb, scale=0.125,
                                                        accum_out=rsum[:, i:i + 1]),
                     reads=[rPB[sb_], rPB[sb_ + 1], rnb], writes=[rE4[k % 2], rrs])
            S.op("act", lambda E: E.activation(sk.rearrange("p (h c) -> p h c", h=2),
                                               SINK[:, hq * 4:hq * 4 + 4].rearrange("p (c h) -> p h c", h=2),
                                               AF.Exp, bias=nb, scale=1.0),
                 reads=[rnb, rCONST], writes=[rsk])
            S.op("dve", lambda E: E.tensor_tensor(sk, sk, rsum, ALU.add), reads=[rsk, rrs], writes=[rsk])
            S.op("dve", lambda E: E.reciprocal(sk, sk), reads=[rsk], writes=[rsk])
            S.op("dve", lambda E: E.tensor_tensor(e4v, e4v, sk.unsqueeze(2).broadcast_to([128, 4, 256]), ALU.mult),
                 reads=[rE4[k % 2], rsk], writes=[rE4[k % 2]])
            dfill(4)

        def stage_b(hq, n, k):
            g = hq // 2
            e4v = E4[k % 2][:, :].rearrange("p (i n) -> p i n", i=4)
            ptp = bank_bf(5).rearrange("p (kb i n) -> p kb i n", kb=2, i=4)
            for i in range(4):
                for kb in range(2):
                    S.op("pe", lambda E, i=i, kb=kb: E.transpose(ptp[:, kb, i, :], e4v[:, i, kb * 128:(kb + 1) * 128], IDENT[:, :]),
                         reads=[rE4[k % 2], rCONST], writes=[rPB[5]], mark=(i == 3 and kb == 1))
            pt = PT[k % 2]
            S.op("act", lambda E: E.activation(pt[:, :], bank_bf(5), AF.Identity), reads=[rPB[5]], writes=[rPT[k % 2]])
            ptv = pt[:, :].rearrange("p (kb n) -> p kb n", kb=2)
            for kb in range(2):
                S.op("pe", lambda E, kb=kb: E.matmul(bank(2), VDv[:, n + kb, g, :, :].rearrange("p o d -> p (o d)"), ptv[:, kb, :],
                                                     start=(kb == 0), stop=(kb == 1)),
                     reads=[rVD, rPT[k % 2]], writes=[rPB[2]], mark=(kb == 1))
            ov = bank(2).rearrange("p (i n) -> p i n", i=4)
            for hf in range(2):
                p0 = hf * 64
                S.op("act", lambda E, hf=hf, p0=p0: E.activation(
                    AT[p0:p0 + 64, 2 * hq:2 * hq + 2, n * 128:(n + 1) * 128],
                    ov[p0:p0 + 64, 2 * hf:2 * hf + 2, :], AF.Identity),
                     reads=[rPB[2]], writes=[rAT])
            dfill(3)

        prev = None
        for n in range(4):
            idx = qa_k["k"]
            qa_k["k"] += 1
            stage_a(hq, n, idx)
            slot_fill()
            if prev is not None:
                stage_b(*prev)
                slot_fill()
            prev = (hq, n, idx)
        stage_b(*prev)
        pull_gen(dfiller, 1000)
        pull(64)

    def merge_gates(mp):
        hrhs = lambda k: HT[:, k, :]
        ti = mp % 2
        tcv = TC[ti][:, :].rearrange("p (j n) -> p j n", j=2)
        tdv = E4[ti][:, :].rearrange("p (j n) -> p j n", j=2)
        units = [unit_fm(w_in_r, u * 16, OFF_GC + mp * 256) for u in range(2)]
        banks = [next_pa(), next_pa()]
        proj_fm(units, 32, hrhs, rR2, TT, banks)
        for j in range(2):
            S.op("act", lambda E, j=j, b=banks[j], tcv=tcv: E.activation(tcv[:, j, :], bank(b), AF.Tanh, scale=0.5),
                 reads=[rPB[banks[j]]], writes=[rTC[ti]])
        units = [unit_fm(w_in_r, u * 16, OFF_GA + mp * 256) for u in range(2)]
        banks = [next_pa(), next_pa()]
        proj_fm(units, 32, hrhs, rR2, TT, banks)
        for j in range(2):
            S.op("act", lambda E, j=j, b=banks[j], tdv=tdv: E.activation(tdv[:, j, :], bank(b), AF.Tanh, scale=0.5),
                 reads=[rPB[banks[j]]], writes=[rE4[ti]])

    def merge_outs(mp):
        zrhs = lambda k: ZT[:, k, :]
        arhs = lambda k: AT[:, k, :]
        ti = mp % 2
        tcv = TC[ti][:, :].rearrange("p (j n) -> p j n", j=2)
        tdv = E4[ti][:, :].rearrange("p (j n) -> p j n", j=2)
        units = [unit_fm(w_co_r, 0, mp * 256)]
        banks = [next_pa(), next_pa()]
        proj_fm(units, 16, zrhs, rRX[2], TT, banks)
        uu = [TA[ti][:, :], PT[ti][:, :].bitcast(F32)]
        ru = [rTA[ti], rPT[ti]]
        for j in range(2):
            S.op("dve", lambda E, j=j, b=banks[j], uu=uu, tcv=tcv: E.scalar_tensor_tensor(uu[j], tcv[:, j, :], 1.0, bank(b), ALU.add, ALU.mult),
                 reads=[rTC[ti], rPB[banks[j]]], writes=[ru[j]])
        units = [unit_fm(w_ao_r, 0, mp * 256)]
        banks = [next_pa(), next_pa()]
        proj_fm(units, 16, arhs, rRX[3], TT, banks)
        ww = [GLU[0][:, 0:TT], GLU[1][:, 0:TT]]
        rw = [rGLU[0], rGLU[1]]
        for j in range(2):
            m = mp * 2 + j
            S.op("dve", lambda E, j=j, b=banks[j], ww=ww, tdv=tdv: E.scalar_tensor_tensor(ww[j], tdv[:, j, :], 1.0, bank(b), ALU.add, ALU.mult),
                 reads=[rE4[ti], rPB[banks[j]]], writes=[rw[j]])
            S.op("dve", lambda E, j=j, m=m, uu=uu, ww=ww: E.tensor_tensor(MG[:, m, :], uu[j], ww[j], ALU.add),
                 reads=[ru[j], rw[j]], writes=[rMGc[m]])

    def phase_merge():
        for mp in range(16):
            if mp + 1 < 16:
                merge_gates(mp + 1)
            merge_outs(mp)

    def phase_wout(t):
        for s in range(2, 4):
            r0 = t * TT + s * 128
            S.op("sp", lambda E, s=s, r0=r0: E.dma_start(out=X1[:, s, :], in_=x_d[r0:r0 + 128, :]), writes=[rRX[s]], dma="xl")
        for db in range(8):
            banks = [(db % 2) * 4 + s for s in range(4)]
            for ku in range(4):
                slot, res = unit_tm(w_out_r, ku * 8, db * 512)
                uv = slot[:, :].rearrange("p (kc n) -> p kc n", kc=8)
                for s in range(4):
                    for kc in range(8):
                        k = ku * 8 + kc
                        S.op("pe", lambda E, s=s, kc=kc, k=k, uv=uv, b=banks[s]:
                             E.matmul(bank(b), MG[:, k, s * 128:(s + 1) * 128], uv[:, kc, :], start=(k == 0), stop=(k == 31)),
                             reads=[res, rMG], writes=[rPB[banks[s]]], mark=(kc == 7))
            for s in range(4):
                xp = X1[:, s, db * 512:(db + 1) * 512]
                S.op("dve", lambda E, xp=xp, b=banks[s]: E.scalar_tensor_tensor(xp, bank(b), 0.5, xp, ALU.mult, ALU.add),
                     reads=[rPB[banks[s]]], writes=[rXP[s][db]])

    def phase_mlp():
        hrhs = lambda k: HT[:, k, :]
        for g in range(8):
            for fb in range(8):
                units = [unit_fm(w_up_r, u * 16, g * 2048 + fb * 256) for u in range(2)]
                banks = [next_pa(), next_pa()]
                proj_fm(units, 32, hrhs, rR2, TT, banks)
                for j in range(2):
                    fc = fb * 2 + j
                    ti = fc % 2
                    r = TA[ti][:, :]
                    S.op("act", lambda E, r=r, b=banks[j]: E.activation(r, bank(b), AF.Relu), reads=[rPB[banks[j]]], writes=[rTA[ti]])
                    S.op("dve", lambda E, r=r, fc=fc: E.tensor_tensor(MG[:, fc, :], r, r, ALU.mult), reads=[rTA[ti]], writes=[rMGc[fc]])
            for db in range(8):
                banks = [4 + s for s in range(4)]
                for ku in range(2):
                    slot, res = unit_tm(w_dn_r, g * 16 + ku * 8, db * 512)
                    uv = slot[:, :].rearrange("p (kc n) -> p kc n", kc=8)
                    for s in range(4):
                        for kc in range(8):
                            k = ku * 8 + kc
                            S.op("pe", lambda E, s=s, kc=kc, k=k, uv=uv, b=banks[s]:
                                 E.matmul(bank(b), MG[:, k, s * 128:(s + 1) * 128], uv[:, kc, :], start=(k == 0), stop=(k == 15)),
                                 reads=[res, rMGc[k]], writes=[rPB[banks[s]]], mark=(kc == 7))
                for s in range(4):
                    xp = X1[:, s, db * 512:(db + 1) * 512]
                    S.op("dve", lambda E, xp=xp, b=banks[s]: E.tensor_tensor(xp, xp, bank(b), ALU.add),
                         reads=[rPB[banks[s]]], writes=[rXP[s][db]])

    def phase_final(t):
        ld(GFIN, gfin_d[:, :], rR2, key="gf")
        for s in range(4):
            ss = STAT[:, s:s + 1]
            sd = STAT[:, 8 + s:9 + s]
            rs = STAT[:, 16 + s:17 + s]
            S.op("act", lambda E, s=s, ss=ss: E.activation(XS[:, :], X1[:, s, :], AF.Square, accum_out=ss),
                 reads=[rRX[s]], writes=XSR + [rSS[s]])
            S.op("act", lambda E, ss=ss, sd=sd: E.activation(sd, ss, AF.Sqrt, bias=EPS_COL[:, 0:1], scale=1.0 / D),
                 reads=[rSS[s], rCONST], writes=[rSD[s]])
            S.op("dve", lambda E, sd=sd, rs=rs: E.reciprocal(rs, sd), reads=[rSD[s]], writes=[rRS[s]])
            S.op("dve", lambda E, s=s, rs=rs: E.scalar_tensor_tensor(X1[:, s, :], X1[:, s, :], rs, GFIN, ALU.mult, ALU.mult),
                 reads=[rRS[s], rR2], writes=[rRX[s]])
            r0 = t * TT + s * 128
            S.op("sp", lambda E, s=s, r0=r0: E.dma_start(out=y_d[r0:r0 + 128, :], in_=X1[:, s, :]), reads=[rRX[s]], dma="st")

    EPS_COL = STAT[:, 60:61]
    S.op("dve", lambda E: E.memset(EPS_COL, EPS), writes=[rCONST])

    import os as _os
    stop = dbg or ""
    if stop == "io":
        for t in range(NT):
            for s_ in range(4):
                r0 = t * TT + s_ * 128
                S.op("sp", lambda E, s_=s_, r0=r0: E.dma_start(out=X1[:, s_, :], in_=x_d[r0:r0 + 128, :]), writes=[rRX[s_]], dma="xl")
                S.op("sp", lambda E, s_=s_, r0=r0: E.dma_start(out=y_d[r0:r0 + 128, :], in_=X1[:, s_, :]), reads=[rRX[s_]], dma="st")
    if stop != "io":
        S.op("sp", lambda E: E.dma_start(out=X1[:, 0, :], in_=xh_d[:, :]), writes=[rRX[0]], dma="xl")
        ldm(COS[:, 0:128], cos_d[:, 0:128], CSR[0:2], key="cs")
        ldm(SIN[:, 0:128], sin_d[:, 0:128], CSR[2:4], key="cs")
        rms_to_fm(1, G1)
        if stop != "rms":
            phase_kv(128, 1, 0, 0)
        if stop not in ("rms", "kv"):
            for cp in range(8):
                conv_pair_proj(cp, 128, True)

    for t in range(NT if stop not in ("io", "rms", "kv", "conv") else 0):
        for s in range(4):
            r0 = t * TT + s * 128
            S.op("sp", lambda E, s=s, r0=r0: E.dma_start(out=X1[:, s, :], in_=x_d[r0:r0 + 128, :]), writes=[rRX[s]], dma="xl")
        ws_state["u"] = 0
        ws_state["mode"] = "cast"
        c0 = 128 + t * TT
        ldm(COS[:, :], cos_d[:, c0:c0 + TT], CSR[0:2], key="cs")
        ldm(SIN[:, :], sin_d[:, c0:c0 + TT], CSR[2:4], key="cs")
        rms_to_fm(4, G1)
        phase_kv(TT, 4, 128, 1)
        conv_pair_proj(0, TT, False)
        for _ in qproj_gen(0):
            pass
        for i in range(8):
            taps = conv_pair_taps_gen(i)
            fill = chain_gens(qproj_gen(i + 1), conv_pair_proj_gen(i + 1, TT, False)) if i < 7 else None
            quad_attn(i, t == 0, fill, taps)
            conv_pair_stats(i)
        merge_gates(0)
        phase_ln()
        if stop not in ("qa",):
            for s in range(2):
                r0 = t * TT + s * 128
                S.op("sp", lambda E, s=s, r0=r0: E.dma_start(out=X1[:, s, :], in_=x_d[r0:r0 + 128, :]), writes=[rRX[s]], dma="xl")
        if stop == "qa":
            break
        phase_merge()
        if stop == "merge":
            break
        phase_wout(t)
        if stop == "wout":
            break
        rms_to_fm(4, G2)
        phase_mlp()
        phase_final(t)
        if t + 1 < NT:
            S.op("act", lambda E: E.activation(KTv[:, :, 0:128], KTv[:, :, 512:640], AF.Identity), reads=[rKT], writes=[rKT])
            S.op("act", lambda E: E.activation(VD[:, 0:512], VD[:, 2048:2560], AF.Identity), reads=[rVD], writes=[rVD])

    if stop not in ("", "io"):
        for q_ in range(4):
            S.op("sp", lambda E, q_=q_: E.dma_start(out=y_d[q_ * 128:(q_ + 1) * 128, :], in_=X1[:, q_, :]),
                 reads=[rRX[q_]], dma="st")
    if S.cnt["st"] > 0:
        S.lists["sp"].append(lambda E: E.wait_ge(S.sems["st"], S.cnt["st"]))

    keys = list(ENGS) + sorted(S.dma_keys)
    for k in keys:
        S.sems[k] = es.enter_context(nc.semaphore(f"sem_{k}"))
    with nc.Block() as block:
        @block.tensor
        def _(E):
            for f in S.lists["pe"]:
                f(E)

        @block.scalar
        def _(E):
            for f in S.lists["act"]:
                f(E)

        @block.vector
        def _(E):
            for f in S.lists["dve"]:
                f(E)

        @block.gpsimd
        def _(E):
            for f in S.lists["pool"]:
                f(E)

        @block.sync
        def _(E):
            for f in S.lists["sp"]:
                f(E)
    es.close()
    return nc


def _host_consts(ntok_core, start_pos):
    half = 32
    inv_freq = (10000.0 ** (-np.arange(0, half, dtype=np.float32) / half)).astype(np.float32)
    pos = np.arange(start_pos - 128, start_pos + ntok_core, dtype=np.float32)
    ang = pos[None, :] * inv_freq[:, None]
    cos32 = np.cos(ang).astype(np.float32)
    sin32 = np.sin(ang).astype(np.float32)
    cos = np.concatenate([cos32, cos32, cos32, cos32], axis=0)
    sin = np.concatenate([-sin32, sin32, -sin32, sin32], axis=0)
    return np.ascontiguousarray(cos), np.ascontiguousarray(sin)


def _masks():
    q = np.arange(128)[:, None]
    kk = np.arange(256)[None, :]
    valid = (kk > q) & (kk <= q + 128)
    mn = np.where(valid, 0.0, MASKNEG).astype(np.float32)
    m0 = mn.copy()
    m0[:, :128] = MASKNEG
    return mn.astype(ml_dtypes.bfloat16), m0.astype(ml_dtypes.bfloat16)


_PROG_CACHE = {}
_DEBUG_HOOK = None
_DEBUG_MODE = None


def kernel(x, norm_mix_g, w_in, b_glu, w_dw, b_dw, conv_ln_g, conv_ln_b, w_conv_out,
           sinks, w_attn_out, w_out, norm_mlp_g, w_mlp_up, w_mlp_down, norm_final_g):
    x = np.asarray(x, dtype=np.float32)
    B, SEQ, _ = x.shape
    ncores = 8
    per_seq = ncores // B
    ntok = SEQ // per_seq
    NT = ntok // TT
    f32 = lambda a: np.ascontiguousarray(np.asarray(a, dtype=np.float32))

    if NT not in _PROG_CACHE:
        _PROG_CACHE[NT] = build_program(NT, dbg=_DEBUG_MODE)
    nc = _PROG_CACHE[NT]

    bf = ml_dtypes.bfloat16
    ident = np.eye(128, dtype=np.float32).astype(bf)
    perm = np.array([(m % 64 + 32) % 64 + 64 * (m // 64) for m in range(128)])
    rperm = np.zeros((128, 128), np.float32)
    rperm[perm, np.arange(128)] = 1.0
    rperm = rperm.astype(bf)
    ones = np.ones((128, 128), np.float32).astype(bf)
    maskn, mask0_first = _masks()

    pc = np.zeros((128, PC_N), np.float32)
    pc[:, PC_G1:PC_G1 + 32] = f32(norm_mix_g).reshape(32, 128).T
    pc[:, PC_G2:PC_G2 + 32] = f32(norm_mlp_g).reshape(32, 128).T
    pc[:, PC_BGLU:PC_BGLU + 32] = f32(b_glu).reshape(32, 128).T
    pc[:, PC_BDW:PC_BDW + 16] = f32(b_dw).reshape(16, 128).T
    pc[:, PC_LNG:PC_LNG + 16] = f32(conv_ln_g).reshape(16, 128).T
    pc[:, PC_LNB:PC_LNB + 16] = f32(conv_ln_b).reshape(16, 128).T
    pc[:, PC_SINK:PC_SINK + 32] = np.broadcast_to(f32(sinks)[None, :], (128, 32))
    wd = f32(w_dw).reshape(CW, 16, 128)
    pc[:, PC_WDW:PC_WDW + 496] = np.transpose(wd, (2, 1, 0)).reshape(128, 496)
    gfin = np.ascontiguousarray(np.broadcast_to(f32(norm_final_g)[None, :], (128, D)))

    shared = {
        "maskn": maskn, "ident": ident, "rperm": rperm, "ones": ones, "gfin": gfin,
        "w_in": f32(w_in), "w_conv_out": f32(w_conv_out), "w_attn_out": f32(w_attn_out),
        "w_out": f32(w_out), "w_mlp_up": f32(w_mlp_up), "w_mlp_down": f32(w_mlp_down),
    }
    _fake = bool(_DEBUG_MODE) and _DEBUG_MODE.endswith("_fake")
    need = _needed_weights(_DEBUG_MODE) if not _fake else ()
    for wn in ALLW:
        if wn not in need:
            shared[wn] = (np.random.RandomState(1).randn(4096, 512).astype(np.float32) / 64) if _fake else np.zeros((128, 128), np.float32)
    in_maps = []
    for c in range(ncores):
        b, hidx = c // per_seq, c % per_seq
        t0 = hidx * ntok
        xc = np.ascontiguousarray(x[b, t0:t0 + ntok, :])
        if hidx == 0:
            xh = np.zeros((128, D), np.float32)
        else:
            xh = np.ascontiguousarray(x[b, t0 - 128:t0, :])
        cos, sin = _host_consts(ntok, t0)
        pcc = pc.copy()
        pcc[:, PC_FLAG] = 0.0 if hidx == 0 else 1.0
        m = dict(shared)
        m.update({"x": xc, "xh": xh, "cos_t": cos, "sin_t": sin,
                  "mask0": mask0_first if hidx == 0 else maskn, "pcols": pcc})
        in_maps.append(m)
    if _DEBUG_HOOK is not None:
        return _DEBUG_HOOK(nc, in_maps)
    res = run_bass_kernel_spmd(nc, in_maps, core_ids=list(range(ncores)))
    out = np.empty((B, SEQ, D), np.float32)
    for c in range(ncores):
        b, hidx = c // per_seq, c % per_seq
        out[b, hidx * ntok:(hidx + 1) * ntok, :] = np.asarray(res.results[c]["y"], dtype=np.float32)
    return out
```
